# Optimizing a Trainium2 kernel written in Bass

```python
import math
import jax, jax.numpy as jnp
from jax import lax
import numpy as np

D_MODEL = 1024
BATCH = 8
SEQ = 8192
DEPTH = 4

GRID_W = 64
CTX_LEN = 256
N_MIXERS = 2
HEAD_DIM = 64
N_HEADS = D_MODEL // HEAD_DIM
N_RWKV = (DEPTH + 1) // 2
N_NA = DEPTH // 2
N_LERP = 6
N_DIRS = 2
D_DECAY_LORA = 64
D_AAA_LORA = 64
D_MV_LORA = 32
LNX_EPS = 64e-5
WIN_H = 8
WIN_W = 16
Q_BLOCK_W = 8
BAND_W = Q_BLOCK_W + WIN_W
N_CBLOCKS = GRID_W // Q_BLOCK_W
RMS_EPS = 1e-6
NEG_INF = -1e30

kernel_name = "hybrid_rwkv7_natten_dit"


def _rmsnorm(x, g):
    xf = x.astype(jnp.float32)
    y = xf * lax.rsqrt(jnp.mean(xf * xf, axis=-1, keepdims=True) + RMS_EPS)
    return (y * g.astype(jnp.float32)).astype(x.dtype)


def _heads(t):
    return t.reshape(*t.shape[:-1], N_HEADS, HEAD_DIM)


def _centred_shift(h):
    pad = jnp.pad(h, ((0, 0), (1, 1), (0, 0)))
    return 0.5 * (pad[:, :-2] + pad[:, 2:]) - h


def _rwkv_project(h, mu, w_rkvg, w0, w1, w2, a0, a1, a2, k_k, k_a, v_first, vres):
    f32 = jnp.float32
    xx = _centred_shift(h)
    xr, xw, xk, xv, xa, xg = [h + xx * mu[n] for n in range(N_LERP)]
    r = xr @ w_rkvg[0]
    k = xk @ w_rkvg[1]
    v = xv @ w_rkvg[2]
    g = jax.nn.silu(xg @ w_rkvg[3])
    if vres is not None:
        v0, v1, v2 = vres
        v = v + (v_first - v) * jax.nn.sigmoid(v0 + (xv @ v1) @ v2)
    lora_w = jnp.einsum('dbtr,drc->dbtc', jnp.tanh(jnp.einsum('btc,dcr->dbtr', xw, w1)), w2)
    w_log = -jax.nn.softplus(-(w0[:, None, None, :] + lora_w).astype(f32)) - 0.5
    decay = jnp.exp(-jnp.exp(w_log))
    lora_a = jnp.einsum('dbtr,drc->dbtc', jnp.einsum('btc,dcr->dbtr', xa, a1), a2)
    a = jax.nn.sigmoid((a0[:, None, None, :] + lora_a).astype(f32))
    kf = k.astype(f32)
    kk = _heads(kf * k_k)
    kk = kk / jnp.maximum(jnp.sqrt(jnp.sum(kk * kk, axis=-1, keepdims=True)), 1e-12)
    k_dir = kf[None] * (1.0 + (a - 1.0) * k_a)
    return (_heads(r.astype(f32)), _heads(k_dir), _heads(v.astype(f32)), kk,
            _heads(a), _heads(decay), g, v)


def _rwkv_scan(r, w, k, v, kk, a, s0, reverse):
    def step(s, inp):
        r_t, w_t, k_t, v_t, kk_t, a_t = inp
        sa = jnp.einsum('bhvk,bhk->bhv', s, -kk_t)
        s = (s * w_t[:, :, None, :]
             + sa[..., None] * (kk_t * a_t)[:, :, None, :]
             + v_t[..., None] * k_t[:, :, None, :])
        y = jnp.einsum('bhvk,bhk->bhv', s, r_t)
        return s, y
    xs = tuple(jnp.moveaxis(t, 1, 0) for t in (r, w, k, v, kk, a))
    s_final, ys = lax.scan(step, s0, xs, reverse=reverse)
    return jnp.moveaxis(ys, 0, 1), s_final


def _rwkv_output(y, r, k_dir, v, g, r_k, lnx_w, lnx_b, w_out):
    B, T = y.shape[:2]
    mu = jnp.mean(y, axis=-1, keepdims=True)
    var = jnp.mean(jnp.square(y - mu), axis=-1, keepdims=True)
    yn = ((y - mu) * lax.rsqrt(var + LNX_EPS)).reshape(B, T, D_MODEL) * lnx_w + lnx_b
    bonus = jnp.sum(jnp.sum(r[None] * k_dir * r_k, axis=-1, keepdims=True), axis=0) * v
    o = (yn + bonus.reshape(B, T, D_MODEL)).astype(g.dtype)
    return (o * g) @ w_out


def _rwkv_mixer(h_lat, h_ctx, mu, w_rkvg, w0, w1, w2, a0, a1, a2, k_k, k_a, r_k,
                lnx_w, lnx_b, w_out, v_first_lat, v_first_ctx, vres, with_ctx_out):
    B = h_lat.shape[0]
    r_l, k_l, v_l, kk_l, a_l, w_l, g_l, vraw_l = _rwkv_project(
        h_lat, mu, w_rkvg, w0, w1, w2, a0, a1, a2, k_k, k_a, v_first_lat, vres)
    r_c, k_c, v_c, kk_c, a_c, w_c, g_c, vraw_c = _rwkv_project(
        h_ctx, mu, w_rkvg, w0, w1, w2, a0, a1, a2, k_k, k_a, v_first_ctx, vres)
    s0 = jnp.zeros((B, N_HEADS, HEAD_DIM, HEAD_DIM), jnp.float32)
    y_lat = jnp.zeros_like(r_l)
    y_ctx = jnp.zeros_like(r_c)
    for d in range(N_DIRS):
        rev = d == 1
        yc, s_ctx = _rwkv_scan(r_c, w_c[d], k_c[d], v_c, kk_c, a_c[d], s0, rev)
        yl, _ = _rwkv_scan(r_l, w_l[d], k_l[d], v_l, kk_l, a_l[d], s_ctx, rev)
        y_lat = y_lat + yl
        y_ctx = y_ctx + yc
    out_lat = _rwkv_output(y_lat, r_l, k_l, v_l, g_l, r_k, lnx_w, lnx_b, w_out)
    out_ctx = _rwkv_output(y_ctx, r_c, k_c, v_c, g_c, r_k, lnx_w, lnx_b, w_out) if with_ctx_out else None
    return out_lat, out_ctx, vraw_l, vraw_c


def _na_column_tables():
    j = np.arange(GRID_W)
    win_start = np.clip(j - WIN_W // 2, 0, GRID_W - WIN_W)
    cb = np.arange(N_CBLOCKS)
    band_start = np.clip(cb * Q_BLOCK_W - WIN_W // 2, 0, GRID_W - BAND_W)
    band_cols = band_start[:, None] + np.arange(BAND_W)[None, :]
    q_cols = cb[:, None] * Q_BLOCK_W + np.arange(Q_BLOCK_W)[None, :]
    key_c = band_cols[:, None, :]
    qs = win_start[q_cols][:, :, None]
    valid = (key_c >= qs) & (key_c < qs + WIN_W)
    dc = np.clip(key_c - q_cols[:, :, None], -(WIN_W - 1), WIN_W - 1) + (WIN_W - 1)
    return band_cols, valid, dc


def _na_mixer(h_lat, h_ctx, w_in, b_in, rpb, w_out, with_ctx_out):
    B, T, D = h_lat.shape
    n_ctx = h_ctx.shape[1]
    rows = T // GRID_W
    kh = min(WIN_H, rows)
    scale = HEAD_DIM ** -0.5
    q, k, v, g = jnp.split(h_lat @ w_in + b_in, 4, axis=-1)
    q = (q * scale).reshape(B, rows, N_CBLOCKS, Q_BLOCK_W, N_HEADS, HEAD_DIM)
    k = k.reshape(B, rows, GRID_W, N_HEADS, HEAD_DIM)
    v = v.reshape(B, rows, GRID_W, N_HEADS, HEAD_DIM)
    k_c, v_c = jnp.split(h_ctx @ w_in[:, D:3 * D] + b_in[D:3 * D], 2, axis=-1)
    k_c, v_c = _heads(k_c), _heads(v_c)
    band_cols, valid_np, dc = _na_column_tables()
    valid = jnp.asarray(valid_np)[:, :, None, :]

    def row_block(args):
        r, q_r = args
        rs = jnp.clip(r - WIN_H // 2, 0, rows - kh)
        k_band = lax.dynamic_slice_in_dim(k, rs, kh, axis=1)[:, :, band_cols]
        v_band = lax.dynamic_slice_in_dim(v, rs, kh, axis=1)[:, :, band_cols]
        dr = rs - r + jnp.arange(kh) + (WIN_H - 1)
        bias = jnp.transpose(rpb[:, dr][:, :, dc], (0, 2, 3, 1, 4))
        s_win = (jnp.einsum('bcqhd,bicwhd->bhcqiw', q_r, k_band).astype(jnp.float32)
                 + bias.astype(jnp.float32)[None])
        s_win = jnp.where(valid, s_win, NEG_INF).reshape(B, N_HEADS, N_CBLOCKS, Q_BLOCK_W, kh * BAND_W)
        s_ctx = jnp.einsum('bcqhd,bshd->bhcqs', q_r, k_c).astype(jnp.float32)
        p = jax.nn.softmax(jnp.concatenate([s_win, s_ctx], axis=-1), axis=-1).astype(v.dtype)
        p_win = p[..., :kh * BAND_W].reshape(B, N_HEADS, N_CBLOCKS, Q_BLOCK_W, kh, BAND_W)
        p_ctx = p[..., kh * BAND_W:]
        o = (jnp.einsum('bhcqiw,bicwhd->bcqhd', p_win, v_band)
             + jnp.einsum('bhcqs,bshd->bcqhd', p_ctx, v_c))
        return o.reshape(B, GRID_W, D)

    o = lax.map(row_block, (jnp.arange(rows), jnp.moveaxis(q, 1, 0)))
    o = jnp.moveaxis(o, 0, 1).reshape(B, T, D)
    out_lat = (o * jax.nn.silu(g)) @ w_out
    out_ctx = None
    if with_ctx_out:
        q_cx = _heads((h_ctx @ w_in[:, :D] + b_in[:D]) * scale)
        g_cx = h_ctx @ w_in[:, 3 * D:] + b_in[3 * D:]
        s = jnp.einsum('bqhd,bshd->bhqs', q_cx, k_c).astype(jnp.float32)
        p = jax.nn.softmax(s, axis=-1).astype(v_c.dtype)
        o_c = jnp.einsum('bhqs,bshd->bqhd', p, v_c).reshape(B, n_ctx, D)
        out_ctx = (o_c * jax.nn.silu(g_cx)) @ w_out
    return out_lat, out_ctx


def setup_inputs(seed: int = 0) -> dict:
    key = jax.random.key(seed)
    ks = iter(jax.random.split(key, 40))
    D = D_MODEL
    f32 = jnp.float32

    def nrm(shape, s):
        return jax.random.normal(next(ks), shape, f32) * s

    def uni(shape, lo, hi):
        return jax.random.uniform(next(ks), shape, f32, lo, hi)

    return {
        "x": nrm((BATCH, SEQ, D), 1.0),
        "c": nrm((BATCH, D), 1.0),
        "ctx": nrm((BATCH, CTX_LEN, D), 1.0),
        "c_ctx": nrm((D,), 1.0),
        "ada_w": nrm((DEPTH, D, 3 * D), 0.5 * D ** -0.5),
        "ada_b": nrm((DEPTH, 3 * D), 0.02),
        "pre_g": 1.0 + nrm((DEPTH, D), 0.05),
        "post_g": 1.0 + nrm((DEPTH, D), 0.05),
        "rw_mu": uni((N_RWKV, N_LERP, D), 0.0, 1.0),
        "rw_w_rkvg": nrm((N_RWKV, 4, D, D), D ** -0.5),
        "rw_w0": uni((N_RWKV, N_DIRS, D), -6.0, -1.0),
        "rw_w1": nrm((N_RWKV, N_DIRS, D, D_DECAY_LORA), D ** -0.5),
        "rw_w2": nrm((N_RWKV, N_DIRS, D_DECAY_LORA, D), 0.5 * D_DECAY_LORA ** -0.5),
        "rw_a0": nrm((N_RWKV, N_DIRS, D), 0.5),
        "rw_a1": nrm((N_RWKV, N_DIRS, D, D_AAA_LORA), D ** -0.5),
        "rw_a2": nrm((N_RWKV, N_DIRS, D_AAA_LORA, D), 0.5 * D_AAA_LORA ** -0.5),
        "rw_v0": nrm((N_RWKV - 1, D), 0.5),
        "rw_v1": nrm((N_RWKV - 1, D, D_MV_LORA), D ** -0.5),
        "rw_v2": nrm((N_RWKV - 1, D_MV_LORA, D), 0.5 * D_MV_LORA ** -0.5),
        "rw_k_k": 0.85 + nrm((N_RWKV, D), 0.05),
        "rw_k_a": 1.0 + nrm((N_RWKV, D), 0.05),
        "rw_r_k": nrm((N_RWKV, N_HEADS, HEAD_DIM), 0.1),
        "rw_lnx_w": 1.0 + nrm((N_RWKV, D), 0.05),
        "rw_lnx_b": nrm((N_RWKV, D), 0.02),
        "rw_w_out": nrm((N_RWKV, D, D), D ** -0.5),
        "na_w_in": nrm((N_NA, D, 4 * D), D ** -0.5),
        "na_b_in": nrm((N_NA, 4 * D), 0.02),
        "na_rpb": nrm((N_NA, N_HEADS, 2 * WIN_H - 1, 2 * WIN_W - 1), 0.5),
        "na_w_out": nrm((N_NA, D, D), D ** -0.5),
    }


def reference(x, c, ctx, c_ctx, ada_w, ada_b, pre_g, post_g, rw_mu, rw_w_rkvg, rw_w0, rw_w1, rw_w2,
              rw_a0, rw_a1, rw_a2, rw_v0, rw_v1, rw_v2, rw_k_k, rw_k_a, rw_r_k, rw_lnx_w, rw_lnx_b,
              rw_w_out, na_w_in, na_b_in, na_rpb, na_w_out):
    silu_c = jax.nn.silu(c)
    silu_cc = jax.nn.silu(c_ctx)
    v_first_lat = None
    v_first_ctx = None
    for i in range(DEPTH):
        last = i == DEPTH - 1
        j = i // N_MIXERS
        shift, scale, gate = jnp.split(silu_c @ ada_w[i] + ada_b[i], 3, axis=-1)
        shift_c, scale_c, gate_c = jnp.split(silu_cc @ ada_w[i] + ada_b[i], 3, axis=-1)
        h = _rmsnorm(x, pre_g[i]) * (1.0 + scale[:, None]) + shift[:, None]
        hc = _rmsnorm(ctx, pre_g[i]) * (1.0 + scale_c) + shift_c
        if i % N_MIXERS == 0:
            vres = None if j == 0 else (rw_v0[j - 1], rw_v1[j - 1], rw_v2[j - 1])
            out, out_c, v_lat, v_ctx = _rwkv_mixer(
                h, hc, rw_mu[j], rw_w_rkvg[j], rw_w0[j], rw_w1[j], rw_w2[j], rw_a0[j], rw_a1[j],
                rw_a2[j], rw_k_k[j], rw_k_a[j], rw_r_k[j], rw_lnx_w[j], rw_lnx_b[j], rw_w_out[j],
                v_first_lat, v_first_ctx, vres, not last)
            if j == 0:
                v_first_lat, v_first_ctx = v_lat, v_ctx
        else:
            out, out_c = _na_mixer(h, hc, na_w_in[j], na_b_in[j], na_rpb[j], na_w_out[j], not last)
        x = x + gate[:, None] * _rmsnorm(out, post_g[i])
        if not last:
            ctx = ctx + gate_c * _rmsnorm(out_c, post_g[i])
    return x
```

```python
import numpy as np
from contextlib import ExitStack
import concourse.bass as bass
import concourse.mybir as mybir
from concourse.bass_utils import run_bass_kernel_spmd

F32 = mybir.dt.float32
BF16 = mybir.dt.bfloat16
AF = mybir.ActivationFunctionType
ALU = mybir.AluOpType
AX = mybir.AxisListType

D = 1024
T = 8192
TC = 256
H = 16
HD = 64
DEPTH = 4
NEG = -30000.0
SC = 32
NSC = 128 // SC
DBG = {}


class Clock:
    def __init__(self, sem):
        self.sem = sem
        self.val = 0


class Buf:
    __slots__ = ("w", "r")

    def __init__(self):
        self.w = None
        self.r = {}


class Eng:
    def __init__(self, name, h, clock, selfwait=True):
        self.name = name
        self.h = h
        self.clock = clock
        self.known = {}
        self.selfwait = selfwait
        self.dma_clocks = []
        self.dma_i = 0
        self.n = 0


class Ctx:
    def __init__(self, nc, st):
        self.nc = nc
        self.st = st

        def mk(name, h, selfwait=True):
            sem = st.enter_context(nc.semaphore("c_" + name))
            return Eng(name, h, Clock(sem), selfwait)

        self.pe = mk("pe", nc.tensor, selfwait=False)
        self.act = mk("act", nc.scalar)
        self.dve = mk("dve", nc.vector)
        self.pool = mk("pool", nc.gpsimd)
        self.sp = mk("sp", nc.sync)
        for e, n in ((self.sp, 20), (self.pool, 10), (self.act, 6)):
            for i in range(n):
                sem = st.enter_context(nc.semaphore("d_%s%d" % (e.name, i)))
                e.dma_clocks.append(Clock(sem))
        self.out_tokens = []

    def _deps(self, eng, reads, writes):
        deps = {}

        def add(tok):
            c, v = tok
            if deps.get(c, 0) < v:
                deps[c] = v

        for b in reads:
            if b.w is not None:
                add(b.w)
        for b in writes:
            if b.w is not None:
                add(b.w)
            for c, v in b.r.items():
                add((c, v))
        for c, v in deps.items():
            if c is eng.clock and not eng.selfwait:
                continue
            if eng.known.get(c, 0) < v:
                eng.h.wait_ge(c.sem, v)
                eng.known[c] = v
                eng.n += 1

    def _mark(self, tok, reads, writes):
        c, v = tok
        for b in reads:
            if b.r.get(c, 0) < v:
                b.r[c] = v
        for b in writes:
            b.w = tok
            b.r = {}

    def op(self, eng, emit, reads=(), writes=()):
        self._deps(eng, reads, writes)
        ins = emit(eng.h)
        eng.clock.val += 1
        ins.then_inc(eng.clock.sem, 1)
        eng.n += 1
        self._mark((eng.clock, eng.clock.val), reads, writes)

    def dma(self, eng, out, in_, reads=(), writes=(), is_output=False, **kw):
        self._deps(eng, reads, writes)
        clk = eng.dma_clocks[eng.dma_i % len(eng.dma_clocks)]
        eng.dma_i += 1
        if eng.known.get(clk, 0) < clk.val:
            eng.h.wait_ge(clk.sem, clk.val)
            eng.known[clk] = clk.val
        ins = eng.h.dma_start(out=out, in_=in_, **kw)
        clk.val += 16
        ins.then_inc(clk.sem, 16)
        eng.n += 1
        tok = (clk, clk.val)
        self._mark(tok, reads, writes)
        if is_output:
            self.out_tokens.append(tok)

    def barrier(self):
        engs = [self.pe, self.act, self.dve, self.pool, self.sp]
        clocks = [e.clock for e in engs]
        for e in engs:
            clocks += e.dma_clocks
        for e in engs:
            for c in clocks:
                if c.val > 0 and e.known.get(c, 0) < c.val:
                    e.h.wait_ge(c.sem, c.val)
                    e.known[c] = c.val
                    e.n += 1

    def finish(self):
        eng = self.sp
        final = {}
        for c, v in self.out_tokens:
            if final.get(c, 0) < v:
                final[c] = v
        for c, v in final.items():
            eng.h.wait_ge(c.sem, v)


class Ring:
    def __init__(self, aps):
        self.aps = aps
        self.bufs = [Buf() for _ in aps]
        self.i = 0

    def next(self):
        k = self.i % len(self.aps)
        self.i += 1
        return self.aps[k], self.bufs[k]


NA_TILES = [(1, 0), (3, 0), (5, 0), (7, 0), (9, 0), (11, 0), (13, 0), (3, 1), (11, 1)]
NA_RES = [7, 2, 3, 4, 8]


def na_pair_tiles(m):
    if m == 0:
        return [0, 1, 2, 3], 3
    if m == 1:
        return [0, 1, 2, 3], 2
    if m == 62:
        return [60, 61, 62, 63], 1
    if m == 63:
        return [60, 61, 62, 63], 0
    return [m - 2, m - 1, m, m + 1, m + 2], None


def na_index_tables():
    kk = np.arange(128)
    a = kk // 64
    c = kk % 64
    b = kk // 64
    j = kk % 64
    ws = np.clip(j - 8, 0, 48)
    colvalid = (c[:, None] >= ws[None, :]) & (c[:, None] < ws[None, :] + 16)
    dc = np.clip(c[:, None] - j[None, :], -15, 15) + 15
    dr_idx = np.zeros((9, 128, 128), np.int64)
    dc_idx = np.zeros((9, 128, 128), np.int64)
    mask = np.zeros((9, 128, 128), np.float32)
    for t, (delta, masked) in enumerate(NA_TILES):
        dr = delta + a[:, None] - b[None, :]
        valid = colvalid & (dr >= 0) & (dr <= 14)
        if masked and delta == 3:
            valid &= ~((a[:, None] == 0) & (b[None, :] == 1))
        if masked and delta == 11:
            valid &= ((a[:, None] == 0) & (b[None, :] == 1))
        dr_idx[t] = np.clip(dr, 0, 14)
        dc_idx[t] = dc
        mask[t] = np.where(valid, 0.0, NEG)
    return dr_idx, dc_idx, mask


class Prog:
    def __init__(self, layers=(0, 1, 2, 3)):
        self.layers = tuple(layers)
        self.nc = bass.Bass("TRN2", target_bir_lowering=False)
        self.st = ExitStack()
        self.K = None

    def din(self, name, shape, dt=F32):
        return self.nc.dram_tensor(name, list(shape), dt, kind="ExternalInput").ap()

    def dscr(self, name, shape, dt=F32):
        return self.nc.dram_tensor(name, list(shape), dt).ap()

    def sb(self, name, shape, dt, st=None):
        self.uid = getattr(self, "uid", 0) + 1
        return (st or self.st).enter_context(self.nc.sbuf_tensor("s%d_%s" % (self.uid, name), list(shape), dt))

    def ps(self, name, shape, dt, st=None):
        self.uid = getattr(self, "uid", 0) + 1
        return (st or self.st).enter_context(self.nc.psum_tensor("p%d_%s" % (self.uid, name), list(shape), dt))

    def build(self):
        nc = self.nc
        with self.st:
            self.K = Ctx(nc, self.st)
            self.declare()
            self.consts()
            self.phase0()
            x_src, c_src = self.x_in, self.ctx_in
            nl = len(self.layers)
            for li, i in enumerate(self.layers):
                last = li == nl - 1
                x_dst = self.y_out if last else self.xs[li % 2]
                c_dst = None if last else self.cs[li % 2]
                self.prenorm(i, x_src, c_src)
                if i % 2 == 1:
                    self.na_layer(i, x_src, c_src, x_dst, c_dst)
                else:
                    self.rwkv_layer(i, x_src, c_src, x_dst, c_dst)
                x_src, c_src = x_dst, c_dst
            self.K.finish()
        return nc

    def declare(self):
        d = self.din
        self.x_in = d("x", [T, D])
        self.ctx_in = d("ctx", [TC, D])
        self.c_fm = d("c_fm", [128, 8, 2])
        self.ada_w = d("ada_w", [DEPTH, D, 3 * D])
        self.ada_b_fm = d("ada_b_fm", [128, DEPTH, 16])
        self.ada_b_gate = d("ada_b_gate", [DEPTH, 1, D])
        self.pre_g_fm = d("pre_g_fm", [128, DEPTH, 8])
        self.post_g_row = d("post_g_row", [DEPTH, 1, D])
        self.na_w_in = d("na_w_in", [2, D, 4 * D])
        self.na_bqk = d("na_bqk", [2, 8, 256])
        self.na_bvg = d("na_bvg", [2, 4, 512])
        self.na_w_out = d("na_w_out", [2, D, D])
        self.na_bias_g = d("na_bias_g", [2, 128, H, 9, 128])
        self.na_mask = d("na_mask", [128, 9, 128])
        self.rw_w_rkvg = d("rw_w_rkvg", [2, 4, D, D])
        self.rw_w_out = d("rw_w_out", [2, D, D])
        self.rw_w1pad = d("rw_w1pad", [2, 2, 128, 8, 128])
        self.rw_a1pad = d("rw_a1pad", [2, 2, 128, 8, 128])
        self.rw_w2pad = d("rw_w2pad", [2, 2, 128, D])
        self.rw_a2pad = d("rw_a2pad", [2, 2, 128, D])
        self.rw_v1pad = d("rw_v1pad", [128, 8, 128])
        self.rw_v2pad = d("rw_v2pad", [128, D])
        self.rw_mu_fm = d("rw_mu_fm", [2, 128, 6, 8])
        self.rw_w0_fm = d("rw_w0_fm", [2, 2, 128, 8])
        self.rw_a0_fm = d("rw_a0_fm", [2, 2, 128, 8])
        self.rw_kk_fm = d("rw_kk_fm", [2, 128, 8])
        self.rw_ka_fm = d("rw_ka_fm", [2, 128, 8])
        self.rw_rk_fm = d("rw_rk_fm", [2, 128, 8])
        self.rw_lnx_w = d("rw_lnx_w", [2, 1, D])
        self.rw_lnx_b = d("rw_lnx_b", [2, 1, D])
        self.rw_masks = d("rw_masks", [2, 128, 5, 128])
        self.rw_seg = d("rw_seg", [2, 128, 8, 128])
        self.rw_blk2 = d("rw_blk2", [128, 128])
        self.rw_blkc = d("rw_blkc", [128, 2])
        self.rw_rowmask = d("rw_rowmask", [128, NSC])
        self.y_out = nc_out = self.nc.dram_tensor("y", [T, D], F32, kind="ExternalOutput").ap()
        s = self.dscr
        self.xs = [s("xs0", [T, D]), s("xs1", [T, D])]
        self.cs = [s("cs0", [TC, D]), s("cs1", [TC, D])]
        self.hT = s("hT", [8, 128, T], BF16)
        self.hcT = s("hcT", [8, 128, TC], BF16)
        self.gb = s("gb", [DEPTH, 2, 128, D])
        self.bmd = s("bmd", [128, H, 9, 128], BF16)
        if DBG.get("DUMP"):
            s = lambda name, shape, dt=F32: self.nc.dram_tensor(name, list(shape), dt, kind="ExternalOutput").ap()
        self.ysc = s("ysc", [T + TC, D])
        self.bonsc = s("bonsc", [2, T + TC, 16])
        self.vf = s("vf", [T + TC, D], BF16)
        s = self.dscr
        self.vsc = s("vsc", [T + TC, D], BF16)
        self.b_x = {}

    def dbuf(self, key):
        if key not in self.b_x:
            self.b_x[key] = Buf()
        return self.b_x[key]

    def consts(self):
        K, nc = self.K, self.nc
        self.ident = self.sb("ident", [128, 128], BF16)
        self.identf = self.sb("identf", [128, 128], F32)
        self.b_const = Buf()
        K.op(K.pool, lambda e: e.memset(self.identf[:], 1.0), writes=[self.b_const])
        K.op(K.pool, lambda e: e.affine_select(
            out=self.identf[:], in_=self.identf[:], pattern=[[-1, 128]],
            compare_op=ALU.is_equal, fill=0.0, base=0, channel_multiplier=1),
            reads=[self.b_const], writes=[self.b_const])
        K.op(K.dve, lambda e: e.tensor_copy(out=self.ident[:], in_=self.identf[:]),
             reads=[self.b_const], writes=[self.b_const])
        self.sel8f = self.sb("sel8f", [8, 8, 128], F32)
        self.sel8 = self.sb("sel8", [8, 8, 128], BF16)
        K.op(K.pool, lambda e: e.memset(self.sel8f[:], 1.0), writes=[self.b_const])
        K.op(K.pool, lambda e: e.affine_select(
            out=self.sel8f[:], in_=self.sel8f[:], pattern=[[-1, 8], [0, 128]],
            compare_op=ALU.is_equal, fill=0.0, base=0, channel_multiplier=1),
            reads=[self.b_const], writes=[self.b_const])
        K.op(K.dve, lambda e: e.tensor_copy(out=self.sel8[:], in_=self.sel8f[:]),
             reads=[self.b_const], writes=[self.b_const])
        self.mhalf = self.sb("mhalf", [128, 1], F32)
        K.op(K.dve, lambda e: e.memset(self.mhalf[:], -0.5), writes=[self.b_const])
        self.sel = self.sb("sel", [2, 2, 128], F32)
        K.op(K.dve, lambda e: e.memset(self.sel[:], 0.0), writes=[self.b_const])
        K.op(K.dve, lambda e: e.memset(self.sel[0:1, 0, :], 1.0), reads=[self.b_const], writes=[self.b_const])
        self.selb = self.sb("selb", [2, 2, 128], F32)

    def phase0(self):
        K, nc = self.K, self.nc
        self.A_fm = self.sb("A_fm", [128, DEPTH, 8, 2], F32)
        self.S_fm = self.sb("S_fm", [128, DEPTH, 8, 2], F32)
        self.b_mod = Buf()
        with ExitStack() as st:
            W = self.sb("adaW", [128, 8, 3 * D], F32, st)
            sc = self.sb("sc2", [128, 8, 2], F32, st)
            bfm = self.sb("adabfm", [128, DEPTH, 16], F32, st)
            pg = self.sb("pregfm", [128, DEPTH, 8], F32, st)
            mod = self.sb("modfm", [128, 16, 2], F32, st)
            grow = self.sb("grow", [2, D], F32, st)
            bg = self.sb("bgate", [2, D], F32, st)
            pgr = self.sb("pgrow", [2, D], F32, st)
            gbt = self.sb("gbt", [128, 2, D], F32, st)
            pfm = self.ps("p0fm", [128, 16, 2], F32, st)
            prow = self.ps("p0row", [2, D], F32, st)
            pbc = self.ps("p0bc", [128, 2, D], F32, st)
            bW, bsc, bsm, bmod, bgrow, bbg, bgbt = (Buf() for _ in range(7))
            bpfm, bprow, bpbc = Buf(), Buf(), Buf()
            K.dma(K.sp, self.sel[1:2, 1, :], self.sel[0:1, 0, :], reads=[self.b_const], writes=[self.b_const])
            K.dma(K.sp, sc[:], self.c_fm[:, :, :], writes=[bsc])
            K.op(K.act, lambda e: e.activation(out=sc[:], in_=sc[:], func=AF.Silu), reads=[bsc], writes=[bsc])
            K.dma(K.sp, bfm[:], self.ada_b_fm[:, :, :], writes=[bsm])
            K.dma(K.sp, pg[:], self.pre_g_fm[:, :, :], writes=[bsm])
            for i in range(DEPTH):
                for kc in range(8):
                    K.dma(K.sp, W[:, kc, :], self.ada_w[i, kc * 128:(kc + 1) * 128, :], writes=[bW])
                K.dma(K.sp, bg[0:1, :], self.ada_b_gate[i], writes=[bbg])
                K.dma(K.sp, bg[1:2, :], self.ada_b_gate[i], writes=[bbg])
                K.dma(K.sp, pgr[0:1, :], self.post_g_row[i], writes=[bbg])
                K.dma(K.sp, pgr[1:2, :], self.post_g_row[i], writes=[bbg])
                for oc in range(16):
                    for kc in range(8):
                        K.op(K.pe, lambda e, oc=oc, kc=kc: e.matmul(
                            pfm[:, oc, :], lhsT=W[:, kc, oc * 128:(oc + 1) * 128], rhs=sc[:, kc, :],
                            start=(kc == 0), stop=(kc == 7)), reads=[bW, bsc], writes=[bpfm])
                for hf in range(2):
                    for kc in range(8):
                        K.op(K.pe, lambda e, hf=hf, kc=kc: e.matmul(
                            prow[:, hf * 512:(hf + 1) * 512], lhsT=sc[:, kc, :],
                            rhs=W[:, kc, 2048 + hf * 512:2048 + (hf + 1) * 512],
                            start=(kc == 0), stop=(kc == 7)), reads=[bW, bsc], writes=[bprow])
                K.op(K.dve, lambda e, i=i: e.tensor_tensor(
                    out=mod[:], in0=pfm[:], in1=bfm[:, i, :].unsqueeze(2).to_broadcast([128, 16, 2]),
                    op=ALU.add), reads=[bpfm, bsm], writes=[bmod])
                K.op(K.dve, lambda e, i=i: e.scalar_tensor_tensor(
                    out=self.A_fm[:, i], in0=mod[:, 8:16, :], scalar=1.0,
                    in1=pg[:, i, :].unsqueeze(2).to_broadcast([128, 8, 2]),
                    op0=ALU.add, op1=ALU.mult), reads=[bmod, bsm], writes=[self.b_mod])
                K.op(K.dve, lambda e, i=i: e.tensor_copy(out=self.S_fm[:, i], in_=mod[:, 0:8, :]),
                     reads=[bmod], writes=[self.b_mod])
                K.op(K.dve, lambda e: e.tensor_tensor(out=grow[:], in0=prow[:], in1=bg[:], op=ALU.add),
                     reads=[bprow, bbg], writes=[bgrow])
                K.op(K.dve, lambda e: e.tensor_tensor(out=grow[:], in0=grow[:], in1=pgr[:], op=ALU.mult),
                     reads=[bgrow, bbg], writes=[bgrow])
                for w in range(2):
                    for hf in range(2):
                        K.op(K.pe, lambda e, w=w, hf=hf: e.matmul(
                            pbc[:, w, hf * 512:(hf + 1) * 512], lhsT=self.sel[:, w, :],
                            rhs=grow[:, hf * 512:(hf + 1) * 512], start=True, stop=True),
                            reads=[bgrow, self.b_const], writes=[bpbc])
                K.op(K.act, lambda e: e.copy(out=gbt[:], in_=pbc[:]), reads=[bpbc], writes=[bgbt])
                for w in range(2):
                    K.dma(K.sp, self.gb[i, w], gbt[:, w, :], reads=[bgbt], writes=[self.dbuf("gb")])
            K.barrier()

    def prenorm(self, i, x_src, c_src):
        K = self.K
        with ExitStack() as st:
            xr = Ring([self.sb("pn_x%d" % k, [128, D], F32, st)[:] for k in range(3)])
            xn = Ring([self.sb("pn_xn%d" % k, [128, D], BF16, st)[:] for k in range(2)])
            junk = self.sb("pn_junk", [128, D], BF16, st)
            bjunk = Buf()
            ssr = Ring([self.sb("pn_ss%d" % k, [128, 2], F32, st)[:] for k in range(3)])
            hg = Ring([self.sb("pn_h%d" % k, [128, 8, 512], BF16, st)[:] for k in range(2)])
            pt = Ring([self.ps("pn_pt%d" % k, [128, 8, 128], BF16, st)[:] for k in range(2)])
            groups = [(c_src, self.hcT, 0, 2, 1, "hcT")]
            for g in range(T // 512):
                groups.append((x_src, self.hT, g * 512, 4, 0, "hT"))
            for src, dst, t0, nsub, w, key in groups:
                h_ap, h_b = hg.next()
                for sub in range(nsub):
                    x_ap, x_b = xr.next()
                    tt = t0 + sub * 128
                    K.dma(K.sp, x_ap, src[tt:tt + 128, :], reads=[self.dbuf(("x", id(src)))], writes=[x_b])
                    ss_ap, ss_b = ssr.next()
                    K.op(K.dve, lambda e, x_ap=x_ap, ss_ap=ss_ap: e.scalar_tensor_tensor(
                        out=junk[:], in0=x_ap, scalar=1.0 / D, in1=x_ap, op0=ALU.mult, op1=ALU.mult,
                        accum_out=ss_ap[:, 0:1]), reads=[x_b], writes=[bjunk, ss_b])
                    K.op(K.dve, lambda e, ss_ap=ss_ap: e.tensor_scalar(
                        out=ss_ap[:, 0:1], in0=ss_ap[:, 0:1], scalar1=1e-6, scalar2=None, op0=ALU.add),
                        reads=[ss_b], writes=[ss_b])
                    K.op(K.pool, lambda e, ss_ap=ss_ap: e.tensor_tensor(
                        out=ss_ap[:, 1:2], in0=ss_ap[:, 0:1], in1=self.mhalf[:], op=ALU.pow),
                        reads=[ss_b, self.b_const], writes=[ss_b])
                    xn_ap, xn_b = xn.next()
                    K.op(K.act, lambda e, x_ap=x_ap, xn_ap=xn_ap, ss_ap=ss_ap: e.activation(
                        out=xn_ap, in_=x_ap, func=AF.Copy, scale=ss_ap[:, 1:2]),
                        reads=[x_b, ss_b], writes=[xn_b])
                    p_ap, p_b = pt.next()
                    for oc in range(8):
                        K.op(K.pe, lambda e, oc=oc, p_ap=p_ap, xn_ap=xn_ap: e.transpose(
                            out=p_ap[:, oc, :], in_=xn_ap[:, oc * 128:(oc + 1) * 128], identity=self.ident[:]),
                            reads=[xn_b, self.b_const], writes=[p_b])
                    for oc in range(8):
                        eng = K.dve if oc % 2 == 0 else K.act
                        if eng is K.dve:
                            K.op(eng, lambda e, oc=oc, p_ap=p_ap, h_ap=h_ap, sub=sub: e.tensor_scalar(
                                out=h_ap[:, oc, sub * 128:(sub + 1) * 128], in0=p_ap[:, oc, :],
                                scalar1=self.A_fm[:, i, oc, w:w + 1], scalar2=self.S_fm[:, i, oc, w:w + 1],
                                op0=ALU.mult, op1=ALU.add), reads=[p_b, self.b_mod], writes=[h_b])
                        else:
                            K.op(eng, lambda e, oc=oc, p_ap=p_ap, h_ap=h_ap, sub=sub: e.activation(
                                out=h_ap[:, oc, sub * 128:(sub + 1) * 128], in_=p_ap[:, oc, :],
                                func=AF.Identity, scale=self.A_fm[:, i, oc, w:w + 1],
                                bias=self.S_fm[:, i, oc, w:w + 1]), reads=[p_b, self.b_mod], writes=[h_b])
                n = nsub * 128
                K.dma(K.sp, dst[:, :, t0:t0 + n].rearrange("c p t -> p c t"), h_ap[:, :, 0:n],
                      reads=[h_b], writes=[self.dbuf(key)])
            K.barrier()

    def post_tile(self, i, w, o_ps_halves, x_src, x_dst, tt, P, is_out):
        K = self.K
        osb_ap, osb_b = P["osb"].next()
        ss_ap, ss_b = P["ss"].next()
        x_ap, x_b = P["xres"].next()
        K.dma(K.sp, x_ap, x_src[tt:tt + 128, :], reads=[self.dbuf(("x", id(x_src)))], writes=[x_b])
        for hf, (p_ap, p_b) in enumerate(o_ps_halves):
            K.op(K.act, lambda e, hf=hf, p_ap=p_ap: e.copy(out=osb_ap[:, hf * 512:(hf + 1) * 512], in_=p_ap),
                 reads=[p_b], writes=[osb_b])
        K.op(K.dve, lambda e: e.scalar_tensor_tensor(
            out=P["junk"][:], in0=osb_ap, scalar=1.0 / D, in1=osb_ap, op0=ALU.mult, op1=ALU.mult,
            accum_out=ss_ap[:, 0:1]), reads=[osb_b], writes=[P["bjunk"], ss_b])
        K.op(K.dve, lambda e: e.tensor_scalar(out=ss_ap[:, 0:1], in0=ss_ap[:, 0:1], scalar1=1e-6, scalar2=None,
                                              op0=ALU.add), reads=[ss_b], writes=[ss_b])
        K.op(K.pool, lambda e: e.tensor_tensor(out=ss_ap[:, 1:2], in0=ss_ap[:, 0:1], in1=self.mhalf[:],
                                               op=ALU.pow), reads=[ss_b, self.b_const], writes=[ss_b])
        K.op(K.dve, lambda e: e.scalar_tensor_tensor(
            out=osb_ap, in0=osb_ap, scalar=ss_ap[:, 1:2], in1=P["gb"][:], op0=ALU.mult, op1=ALU.mult),
            reads=[osb_b, ss_b, P["bgb"]], writes=[osb_b])
        K.op(K.pool, lambda e: e.tensor_tensor(out=x_ap, in0=x_ap, in1=osb_ap, op=ALU.add),
             reads=[x_b, osb_b], writes=[x_b])
        K.dma(K.sp, x_dst[tt:tt + 128, :], x_ap, reads=[x_b], writes=[self.dbuf(("x", id(x_dst)))],
              is_output=is_out)

    def post_alloc(self, i, st):
        K = self.K
        P = {}
        P["osb"] = Ring([self.sb("po_o%d" % k, [128, D], F32, st)[:] for k in range(1)])
        P["ss"] = Ring([self.sb("po_ss%d" % k, [128, 2], F32, st)[:] for k in range(2)])
        P["xres"] = Ring([self.sb("po_x%d" % k, [128, D], F32, st)[:] for k in range(2)])
        P["junk"] = self.sb("po_junk", [128, D], BF16, st)
        P["bjunk"] = Buf()
        P["gb"] = self.sb("po_gb", [128, D], F32, st)
        P["bgb"] = Buf()
        return P

    def na_layer(self, i, x_src, c_src, x_dst, c_dst):
        K, nc = self.K, self.nc
        j = i // 2
        is_last = c_dst is None
        with ExitStack() as st:
            w_in = self.sb("na_win", [128, 8, 4 * D], BF16, st)
            w_out = self.sb("na_wout", [128, 8, D], BF16, st)
            bqk = self.sb("na_bqk", [8, 256], BF16, st)
            bvg = self.sb("na_bvg", [4, 512], BF16, st)
            bm5 = self.sb("na_bm5", [128, H, 5, 128], BF16, st)
            bw = Buf()
            for kc in range(8):
                for q in range(4):
                    K.dma(K.pool, w_in[:, kc, q * 1024:(q + 1) * 1024],
                          self.na_w_in[j, kc * 128:(kc + 1) * 128, q * 1024:(q + 1) * 1024], writes=[bw])
                K.dma(K.pool, w_out[:, kc, :], self.na_w_out[j, kc * 128:(kc + 1) * 128, :], writes=[bw])
            K.dma(K.pool, bqk[:], self.na_bqk[j], writes=[bw])
            K.dma(K.pool, bvg[:], self.na_bvg[j], writes=[bw])
            with ExitStack() as st2:
                bm = self.sb("na_bm", [128, H, 9, 128], BF16, st2)
                stg = Ring([self.sb("na_stg%d" % k, [128, 9, 128], F32, st2)[:] for k in range(2)])
                msk = self.sb("na_msk", [128, 9, 128], F32, st2)
                bmsk, bbm = Buf(), Buf()
                K.dma(K.sp, msk[:], self.na_mask[:, :, :], writes=[bmsk])
                for h in range(H):
                    s_ap, s_b = stg.next()
                    K.dma(K.sp, s_ap, self.na_bias_g[j, :, h, :, :], writes=[s_b])
                    K.op(K.dve, lambda e, h=h, s_ap=s_ap: e.tensor_tensor(
                        out=bm[:, h, :, :], in0=s_ap, in1=msk[:], op=ALU.add), reads=[s_b, bmsk], writes=[bbm])
                for k, tid in enumerate(NA_RES):
                    K.op(K.pool, lambda e, k=k, tid=tid: e.tensor_copy(out=bm5[:, :, k, :], in_=bm[:, :, tid, :]),
                         reads=[bbm], writes=[bw])
                K.dma(K.sp, self.bmd[:, :, :, :], bm[:], reads=[bbm], writes=[self.dbuf("bmd")])
                K.barrier()
            self._na_main(i, j, x_src, c_src, x_dst, c_dst, w_in, w_out, bqk, bvg, bm5, bw, st)
            K.barrier()

    def _na_main(self, i, j, x_src, c_src, x_dst, c_dst, w_in, w_out, bqk, bvg, bm5, bw, st):
        K = self.K
        is_last = c_dst is None
        NS, NQ = 5, 4
        KcT = self.sb("na_KcT", [128, 8, 2, 128], BF16, st)
        Vc = self.sb("na_Vc", [128, 2, H, 66], BF16, st)
        bKc = [Buf() for _ in range(2)]
        bVc = [Buf() for _ in range(2)]
        hring = Ring([self.sb("na_h%d" % k, [128, 8, 128], BF16, st)[:] for k in range(2)])
        PT = Ring([self.sb("na_PT%d" % k, [128, 7, 128], BF16, st)[:] for k in range(2)])
        og = Ring([self.sb("na_og%d" % k, [128, D], BF16, st)[:] for k in range(1)])
        ogT = Ring([self.sb("na_ogT%d" % k, [128, 8, 128], BF16, st)[:] for k in range(1)])
        rc = Ring([self.sb("na_rc%d" % k, [128, H], F32, st)[:] for k in range(2)])
        of = Ring([self.sb("na_of%d" % k, [128, D], BF16, st)[:] for k in range(1)])
        eb = Ring([self.sb("na_eb%d" % k, [128, 4, 128], BF16, st)[:] for k in range(2)])
        P = self.post_alloc(i, st)
        Sps = Ring([self.ps("na_S%d" % k, [128, 8, 128], F32, st)[:] for k in range(2)])
        PVps = Ring([self.ps("na_PV%d" % k, [128, 512], F32, st)[:] for k in range(2)])
        Mps = Ring([self.ps("na_M%d" % k, [128, 512], F32, st)[:] for k in range(2)])
        K.op(K.pool, lambda e: e.memset(Vc[:, :, :, 64:66], 1.0), writes=bVc)

        def project(src, t0, dq, dk, dv, dg):
            h_ap, h_b = hring.next()
            K.dma(K.sp, h_ap, src[:, :, t0:t0 + 128].rearrange("c p t -> p c t"),
                  reads=[self.dbuf("hT"), self.dbuf("hcT")], writes=[h_b])
            for which, dest, col0 in ((0, dq, 0), (1, dk, D)):
                if dest is None:
                    continue
                dfn, dbuf_ = dest
                for g4 in range(2):
                    p_ap, p_b = Mps.next()
                    pv = p_ap.rearrange("p (a b) -> p a b", a=4)
                    for a in range(4):
                        oc = g4 * 4 + a
                        c0 = col0 + oc * 128
                        blk = c0 // 128
                        for kc in range(8):
                            K.op(K.pe, lambda e, a=a, kc=kc, c0=c0, pv=pv: e.matmul(
                                pv[:, a, :], lhsT=w_in[:, kc, c0:c0 + 128], rhs=h_ap[:, kc, :],
                                start=(kc == 0), stop=False), reads=[bw, h_b], writes=[p_b])
                        K.op(K.pe, lambda e, a=a, blk=blk, pv=pv: e.matmul(
                            pv[:, a, :], lhsT=bqk[:, (blk // 8) * 128:(blk // 8) * 128 + 128],
                            rhs=self.sel8[:, blk % 8, :], start=False, stop=True),
                            reads=[bw, self.b_const], writes=[p_b])
                    if which == 0:
                        K.op(K.act, lambda e, pv=pv, g4=g4, dfn=dfn: e.activation(
                            out=dfn(g4, 0), in_=pv[0:64], func=AF.Copy, scale=0.125), reads=[p_b], writes=[dbuf_])
                        K.op(K.act, lambda e, pv=pv, g4=g4, dfn=dfn: e.activation(
                            out=dfn(g4, 1), in_=pv[64:128], func=AF.Copy, scale=0.125), reads=[p_b], writes=[dbuf_])
                    else:
                        K.op(K.dve, lambda e, pv=pv, g4=g4, dfn=dfn: e.tensor_copy(out=dfn(g4), in_=pv),
                             reads=[p_b], writes=[dbuf_])
            for which, dest, col0 in ((2, dv, 2 * D), (3, dg, 3 * D)):
                if dest is None:
                    continue
                dfn, dbuf_ = dest
                for hf in range(2):
                    p_ap, p_b = Mps.next()
                    c0 = col0 + hf * 512
                    blk = (c0 - 2 * D) // 512
                    for kc in range(8):
                        K.op(K.pe, lambda e, kc=kc, c0=c0, p_ap=p_ap: e.matmul(
                            p_ap, lhsT=h_ap[:, kc, :], rhs=w_in[:, kc, c0:c0 + 512],
                            start=(kc == 0), stop=False), reads=[bw, h_b], writes=[p_b])
                    K.op(K.pe, lambda e, blk=blk, p_ap=p_ap: e.matmul(
                        p_ap, lhsT=self.sel8[0:4, blk, :], rhs=bvg[:, :],
                        start=False, stop=True), reads=[bw, self.b_const], writes=[p_b])
                    if which == 2:
                        K.op(K.dve, lambda e, hf=hf, p_ap=p_ap, dfn=dfn: e.tensor_copy(
                            out=dfn(hf), in_=p_ap.rearrange("p (h d) -> p h d", h=8)),
                            reads=[p_b], writes=[dbuf_])
                    else:
                        K.op(K.act, lambda e, hf=hf, p_ap=p_ap, dfn=dfn: e.activation(
                            out=dfn(hf), in_=p_ap, func=AF.Silu), reads=[p_b], writes=[dbuf_])

        def attend(q_fn, q_b, keytiles, edge0, g_ap, g_b, w, xs_, xd_, tt, is_out):
            nt = len(keytiles)
            og_ap, og_b = og.next()
            rc_ap, rc_b = rc.next()
            of_ap, of_b = of.next()
            groups = [(0, 7), (7, 14), (14, 16)]
            for h0, h1 in groups:
                pv_ap, pv_b = PVps.next()
                for h in range(h0, h1):
                    s_ap, s_b = Sps.next()
                    if edge0 is not None:
                        eb_ap, eb_b = eb.next()
                        K.dma(K.sp, eb_ap, self.bmd[:, h, edge0:edge0 + 4, :], reads=[self.dbuf("bmd")],
                              writes=[eb_b])
                    bi = 0
                    for ti, (k_fn, k_b, v_fn, v_b, hb) in enumerate(keytiles):
                        K.op(K.pe, lambda e, ti=ti, k_fn=k_fn, h=h, hb=hb: e.matmul(
                            s_ap[:, ti, :], lhsT=k_fn(h), rhs=q_fn(h), start=True, stop=(not hb)),
                            reads=[k_b, q_b], writes=[s_b])
                        if hb:
                            if edge0 is not None:
                                b_ap, b_b = eb_ap[:, bi, :], eb_b
                            else:
                                b_ap, b_b = bm5[:, h, bi, :], bw
                            bi += 1
                            K.op(K.pe, lambda e, ti=ti, b_ap=b_ap: e.matmul(
                                s_ap[:, ti, :], lhsT=self.ident[:], rhs=b_ap, start=False, stop=True),
                                reads=[b_b, self.b_const], writes=[s_b])
                    pt_ap, pt_b = PT.next()
                    K.op(K.act, lambda e, nt=nt, s_ap=s_ap, pt_ap=pt_ap: e.activation(
                        out=pt_ap[:, 0:nt, :], in_=s_ap[:, 0:nt, :], func=AF.Exp), reads=[s_b], writes=[pt_b])
                    o_sl = pv_ap[:, (h - h0) * 65:(h - h0 + 1) * 65]
                    for ti, (k_fn, k_b, v_fn, v_b, hb) in enumerate(keytiles):
                        K.op(K.pe, lambda e, ti=ti, v_fn=v_fn, h=h, o_sl=o_sl, pt_ap=pt_ap: e.matmul(
                            o_sl, lhsT=pt_ap[:, ti, :], rhs=v_fn(h), start=(ti == 0), stop=(ti == nt - 1)),
                            reads=[pt_b, v_b], writes=[pv_b])
                nh = h1 - h0
                pv3 = pv_ap[:, 0:nh * 65].rearrange("p (h d) -> p h d", d=65)
                K.op(K.dve, lambda e, pv3=pv3, h0=h0, h1=h1: e.reciprocal(
                    out=rc_ap[:, h0:h1].unsqueeze(2), in_=pv3[:, :, 64:65]), reads=[pv_b], writes=[rc_b])
                K.op(K.dve, lambda e, pv3=pv3, h0=h0, h1=h1, nh=nh: e.tensor_tensor(
                    out=of_ap[:, h0 * 64:h1 * 64].rearrange("p (h d) -> p h d", d=64), in0=pv3[:, :, 0:64],
                    in1=rc_ap[:, h0:h1].unsqueeze(2).to_broadcast([128, nh, 64]), op=ALU.mult),
                    reads=[pv_b, rc_b], writes=[of_b])
            K.op(K.pool, lambda e: e.tensor_tensor(out=og_ap, in0=of_ap, in1=g_ap, op=ALU.mult),
                 reads=[of_b, g_b], writes=[og_b])
            t_ap, t_b = Mps.next()
            tv = t_ap.bitcast(BF16).rearrange("p (a b) -> p a b", b=128)[:, 0:8, :]
            for oc in range(8):
                K.op(K.pe, lambda e, oc=oc: e.transpose(out=tv[:, oc, :], in_=og_ap[:, oc * 128:(oc + 1) * 128],
                                                        identity=self.ident[:]),
                     reads=[og_b, self.b_const], writes=[t_b])
            ogT_ap, ogT_b = ogT.next()
            K.op(K.dve, lambda e: e.tensor_copy(out=ogT_ap, in_=tv), reads=[t_b], writes=[ogT_b])
            halves = []
            for hf in range(2):
                p_ap, p_b = Mps.next()
                for kc in range(8):
                    K.op(K.pe, lambda e, kc=kc, hf=hf, p_ap=p_ap: e.matmul(
                        p_ap, lhsT=ogT_ap[:, kc, :], rhs=w_out[:, kc, hf * 512:(hf + 1) * 512],
                        start=(kc == 0), stop=(kc == 7)), reads=[ogT_b, bw], writes=[p_b])
                halves.append((p_ap, p_b))
            self.post_tile(i, w, halves, xs_, xd_, tt, P, is_out)

        def ctx_tiles():
            return [((lambda h, t=t: KcT[:, h // 2, t, :]), bKc[t],
                     (lambda h, t=t: Vc[:, t, h, 0:65]), bVc[t], False) for t in range(2)]

        with ExitStack() as stc:
            if not is_last:
                QcT = self.sb("na_QcT", [128, 8, 2, 2, 128], BF16, stc)
                Gc = self.sb("na_Gc", [128, 2, D], BF16, stc)
                bQc = [Buf() for _ in range(2)]
                bGc = [Buf() for _ in range(2)]
                K.op(K.pool, lambda e: e.memset(QcT[:], 0.0), writes=bQc)
                K.dma(K.sp, P["gb"][:], self.gb[i, 1], reads=[self.dbuf("gb")], writes=[P["bgb"]])
            for t in range(2):
                project(self.hcT, t * 128,
                        None if is_last else (
                            lambda g4, par, t=t: QcT[par * 64:par * 64 + 64, g4 * 4:(g4 + 1) * 4, par, t, :], bQc[t]),
                        (lambda g4, t=t: KcT[:, g4 * 4:(g4 + 1) * 4, t, :], bKc[t]),
                        (lambda hf, t=t: Vc[:, t, hf * 8:(hf + 1) * 8, 0:64], bVc[t]),
                        None if is_last else (lambda hf, t=t: Gc[:, t, hf * 512:(hf + 1) * 512], bGc[t]))
            if not is_last:
                for t in range(2):
                    attend(lambda h, t=t: QcT[:, h // 2, h % 2, t, :], bQc[t],
                           ctx_tiles(), None, Gc[:, t, :], bGc[t], 1, c_src, c_dst, t * 128, False)
                K.barrier()

        KT = self.sb("na_KT", [128, 8, NS, 128], BF16, st)
        VV = self.sb("na_V", [128, NS, H, 66], BF16, st)
        QT = self.sb("na_QT", [128, 8, 2, NQ, 128], BF16, st)
        GG = self.sb("na_G", [128, NQ, D], BF16, st)
        bKT = [Buf() for _ in range(NS)]
        bV = [Buf() for _ in range(NS)]
        bQ = [Buf() for _ in range(NQ)]
        bG = [Buf() for _ in range(NQ)]
        K.op(K.pool, lambda e: e.memset(VV[:, :, :, 64:66], 1.0), writes=bV)
        K.op(K.pool, lambda e: e.memset(QT[:], 0.0), writes=bQ)
        K.dma(K.sp, P["gb"][:], self.gb[i, 0], reads=[self.dbuf("gb")], writes=[P["bgb"]])

        def do_proj(s):
            ks, qs = s % NS, s % NQ
            project(self.hT, s * 128,
                    (lambda g4, par, qs=qs: QT[par * 64:par * 64 + 64, g4 * 4:(g4 + 1) * 4, par, qs, :], bQ[qs]),
                    (lambda g4, ks=ks: KT[:, g4 * 4:(g4 + 1) * 4, ks, :], bKT[ks]),
                    (lambda hf, ks=ks: VV[:, ks, hf * 8:(hf + 1) * 8, 0:64], bV[ks]),
                    (lambda hf, qs=qs: GG[:, qs, hf * 512:(hf + 1) * 512], bG[qs]))

        def do_attn(m):
            kts, edge0 = na_pair_tiles(m)
            tiles = []
            for kt in kts:
                ks = kt % NS
                tiles.append(((lambda h, ks=ks: KT[:, h // 2, ks, :]), bKT[ks],
                              (lambda h, ks=ks: VV[:, ks, h, 0:65]), bV[ks], True))
            tiles += ctx_tiles()
            qs = m % NQ
            attend(lambda h, qs=qs: QT[:, h // 2, h % 2, qs, :], bQ[qs],
                   tiles, edge0, GG[:, qs, :], bG[qs], 0, x_src, x_dst, m * 128, x_dst is self.y_out)

        npairs = T // 128
        projected = 0
        for m in range(npairs):
            need = max(na_pair_tiles(m)[0])
            while projected <= need:
                do_proj(projected)
                projected += 1
            do_attn(m)

    def TT(self, eng, out, in0, in1, op, r, w):
        self.K.op(eng, lambda e: e.tensor_tensor(out=out, in0=in0, in1=in1, op=op), r, w)

    def STT(self, out, in0, scalar, in1, op0, op1, r, w):
        self.K.op(self.K.dve, lambda e: e.scalar_tensor_tensor(out=out, in0=in0, scalar=scalar, in1=in1,
                                                               op0=op0, op1=op1), r, w)

    def TS(self, eng, out, in0, s1, s2, op0, op1, r, w):
        if s2 is None:
            self.K.op(eng, lambda e: e.tensor_scalar(out=out, in0=in0, scalar1=s1, scalar2=None, op0=op0), r, w)
        else:
            self.K.op(eng, lambda e: e.tensor_scalar(out=out, in0=in0, scalar1=s1, scalar2=s2, op0=op0, op1=op1), r, w)

    def ACT(self, out, in_, func, r, w, scale=1.0, bias=0.0):
        self.K.op(self.K.act, lambda e: e.activation(out=out, in_=in_, func=func, scale=scale, bias=bias), r, w)

    def CP(self, eng, out, in_, r, w):
        if eng is self.K.act:
            self.K.op(eng, lambda e: e.copy(out=out, in_=in_), r, w)
        else:
            self.K.op(eng, lambda e: e.tensor_copy(out=out, in_=in_), r, w)

    def MM(self, out, lhsT, rhs, start, stop, r, w):
        self.K.op(self.K.pe, lambda e: e.matmul(out, lhsT=lhsT, rhs=rhs, start=start, stop=stop), r, w)

    def TR(self, out, in_, r, w):
        self.K.op(self.K.pe, lambda e: e.transpose(out=out, in_=in_, identity=self.ident[:]),
                  list(r) + [self.b_const], w)

    def MS(self, eng, ap, val, w):
        self.K.op(eng, lambda e: e.memset(ap, val), (), w)

    def rwkv_layer(self, i, x_src, c_src, x_dst, c_dst):
        j = i // 2
        for d in range(DBG.get("RW_PASSES", 2)):
            self.rwkv_pass(i, j, d)
        if not DBG.get("RW_SKIP_OUT", 0):
            self.rwkv_out(i, j, x_src, c_src, x_dst, c_dst)

    def rw_chunks(self, d):
        order = [("c", 0), ("c", 1)] + [("l", n) for n in range(T // 128)]
        if d == 1:
            order = [("c", 1), ("c", 0)] + [("l", n) for n in reversed(range(T // 128))]
        return order

    def load_hh(self, hhR, seg_, n):
        K = self.K
        src = self.hcT if seg_ == "c" else self.hT
        nseg = 2 if seg_ == "c" else T // 128
        hh, hh_b = hhR.next()
        if n == 0:
            self.MS(K.pool, hh[:, :, 0:1], 0.0, [hh_b])
        if n == nseg - 1:
            self.MS(K.pool, hh[:, :, 129:130], 0.0, [hh_b])
        lo = n * 128 - (1 if n > 0 else 0)
        hi = n * 128 + 128 + (1 if n < nseg - 1 else 0)
        c0 = 0 if n > 0 else 1
        K.dma(K.sp, hh[:, :, c0:c0 + (hi - lo)], src[:, :, lo:hi].rearrange("c p t -> p c t"),
              reads=[self.dbuf("hT"), self.dbuf("hcT")], writes=[hh_b])
        return hh, hh_b

    def rwkv_pass(self, i, j, d):
        K = self.K
        CD = 0.60653066
        with ExitStack() as st:
            sb = lambda name, shape, dt: self.sb(name, shape, dt, st)
            Wr, Wk, Wv = (sb("rw_W%d" % m, [128, 8, D], BF16) for m in range(3))
            w1a1 = sb("rw_w1a1", [128, 8, 2, 128], BF16)
            w2a2 = sb("rw_w2a2", [128, 2, D], BF16)
            bw = Buf()
            for m, W in ((0, Wr), (1, Wk), (2, Wv)):
                for kc in range(8):
                    K.dma(K.pool, W[:, kc, :], self.rw_w_rkvg[j, m, kc * 128:(kc + 1) * 128, :], writes=[bw])
            K.dma(K.pool, w1a1[:, :, 0, :], self.rw_w1pad[j, d], writes=[bw])
            K.dma(K.pool, w1a1[:, :, 1, :], self.rw_a1pad[j, d], writes=[bw])
            K.dma(K.pool, w2a2[:, 0, :], self.rw_w2pad[j, d], writes=[bw])
            K.dma(K.pool, w2a2[:, 1, :], self.rw_a2pad[j, d], writes=[bw])
            if j == 1:
                v1p = sb("rw_v1p", [128, 8, 128], BF16)
                v2p = sb("rw_v2p", [128, D], BF16)
                K.dma(K.pool, v1p[:], self.rw_v1pad[:, :, :], writes=[bw])
                K.dma(K.pool, v2p[:], self.rw_v2pad[:, :], writes=[bw])
                hv = sb("rw_hv", [128, 128], BF16)
                bhv = Buf()
                self.MS(K.pool, hv[:], 0.0, [bhv])
                self.MS(K.pool, hv[32:33, :], 1.0, [bhv])
            prm = sb("rw_prm", [128, 13, 8], F32)
            K.dma(K.sp, prm[:, 0:6, :], self.rw_mu_fm[j], writes=[bw])
            K.dma(K.sp, prm[:, 6, :], self.rw_w0_fm[j, d], writes=[bw])
            K.dma(K.sp, prm[:, 7, :], self.rw_a0_fm[j, d], writes=[bw])
            K.dma(K.sp, prm[:, 8, :], self.rw_kk_fm[j], writes=[bw])
            K.dma(K.sp, prm[:, 9, :], self.rw_ka_fm[j], writes=[bw])
            K.dma(K.sp, prm[:, 10, :], self.rw_rk_fm[j], writes=[bw])
            msk = sb("rw_msk", [128, 5, 128], BF16)
            segm = sb("rw_seg", [128, 8, 128], F32)
            blk2 = sb("rw_blk2", [128, 128], BF16)
            blkc = sb("rw_blkc", [128, 2], BF16)
            K.dma(K.pool, msk[:], self.rw_masks[d], writes=[bw])
            K.dma(K.sp, segm[:], self.rw_seg[d], writes=[bw])
            K.dma(K.pool, blk2[:], self.rw_blk2[:, :], writes=[bw])
            K.dma(K.pool, blkc[:], self.rw_blkc[:, :], writes=[bw])

            def pbc(k):
                return prm[:, k, :].unsqueeze(2).to_broadcast([128, 8, 128])

            hhR = Ring([sb("rw_hh%d" % k, [128, 8, 130], BF16)[:] for k in range(2)])
            xs_t = sb("rw_xs", [128, 8, 128], BF16); b_xs = Buf()
            xx_t = sb("rw_xx", [128, 8, 128], BF16); b_xx = Buf()
            ltmp = sb("rw_ltmp", [128, 8, 128], F32); b_ltmp = Buf()
            xn = [sb("rw_xn%d" % k, [128, 8, 128], BF16) for k in range(5)]
            b_xn = [Buf() for _ in range(5)]
            hwa = sb("rw_hwa", [128, 128], BF16); b_hwa = Buf()
            F = {}
            for nm in ("B1", "B2", "eP", "eN", "ks", "rs", "a", "kk"):
                F[nm] = (sb("rw_f_" + nm, [128, 8, 128], F32), Buf())
            F["rn"] = (ltmp, b_ltmp)
            sq = sb("rw_sq", [128, 8, 128], BF16); b_sq = Buf()
            rkk = sb("rw_rkk", [128, 8, 128], BF16); b_rkk = Buf()
            KTb = sb("rw_KT", [128, 8, 128], BF16); b_KT = Buf()
            BTb = sb("rw_BT", [128, 8, 128], BF16); b_BT = Buf()
            ATb = sb("rw_AT", [128, 8, 128], BF16); b_AT = Buf()
            ARz = sb("rw_ARz", [128, 8, 2, 256], BF16); b_AR = Buf()
            self.MS(K.pool, ARz[:], 0.0, [b_AR])
            Vtm = sb("rw_V", [128, D], BF16); b_V = Buf()
            Btm = sb("rw_Btm", [128, D], BF16); b_Btm = Buf()
            Ktm = sb("rw_Ktm", [128, D], BF16); b_Ktm = Buf()
            Vz = [sb("rw_Vz%d" % k, [128, D], BF16) for k in range(NSC)]; b_Vz = Buf()
            rm = sb("rw_rm", [128, NSC], F32)
            K.dma(K.sp, rm[:], self.rw_rowmask[:, :], writes=[bw])
            X = [sb("rw_X%d" % k, [128, 16, 128], BF16) for k in range(2)]
            b_X = [[Buf() for _ in range(4)] for _ in range(2)]
            MMs = sb("rw_MM", [128, 8, 4, 128], BF16); b_MM = [Buf() for _ in range(8)]
            L0s = sb("rw_L0", [128, 8, 128], BF16); b_L0 = [Buf() for _ in range(2)]
            LN = [sb("rw_LN%d" % k, [128, 8, 2, 128], BF16) for k in range(2)]
            b_LN = [[Buf() for _ in range(4)] for _ in range(2)]
            Wz = sb("rw_Wz", [128, 8, 128], BF16); b_Wz = Buf()
            self.MS(K.pool, Wz[:], 0.0, [b_Wz])
            WTz = sb("rw_WTz", [128, 8, 128], BF16); b_WTz = Buf()
            Us = sb("rw_U", [128, 8, 64], BF16); b_U = Buf()
            ysb = sb("rw_y", [128, D], F32); b_y = Buf()
            Af = sb("rw_Af", [128, 8, 128], F32); b_Af = Buf()
            Ab = sb("rw_Ab", [128, 8, 128], BF16); b_Ab = Buf()
            PC = sb("rw_PC", [128, 8, NSC], F32); b_PC = Buf()
            bon = sb("rw_bon", [128, 16], F32); b_bon = Buf()
            self.MS(K.pool, Af[:], 0.0, [b_Af])
            self.MS(K.pool, Ab[:], 0.0, [b_Ab])
            if j == 1:
                vraw = ysb[:]; b_vraw = b_y
                vft = rkk[:].rearrange("p a b -> p (a b)"); b_vft = b_rkk
                gate = F["eP"][0][:].rearrange("p a b -> p (a b)"); b_gate = F["eP"][1]
            banks = Ring([self.ps("rw_ps%d" % k, [128, 512], F32, st)[:] for k in range(8)])

            NCH = DBG.get("RW_NCHUNK", 1000)
            STAGE = DBG.get("RW_STAGE", 100)
            for seg_, n in self.rw_chunks(d)[:NCH]:
                g0 = (0 if seg_ == "c" else TC) + n * 128
                hh, hh_b = self.load_hh(hhR, seg_, n)
                hc = hh[:, :, 1:129]
                self.TT(K.pool, xs_t[:], hh[:, :, 0:128], hh[:, :, 2:130], ALU.add, [hh_b], [b_xs])
                self.STT(xx_t[:], xs_t[:], 0.5, hc, ALU.mult, ALU.subtract, [b_xs, hh_b], [b_xx])
                for q in range(5):
                    self.TT(K.pool, ltmp[:], xx_t[:], pbc(q), ALU.mult, [b_xx, bw], [b_ltmp])
                    self.TT(K.dve, xn[q][:], ltmp[:], hc, ALU.add, [b_ltmp, hh_b], [b_xn[q]])
                if STAGE < 2:
                    continue
                XR, XW, XK, XV, XA = range(5)
                p_ap, p_b = banks.next()
                for kc in range(8):
                    self.MM(p_ap[:, 0:128], w1a1[:, kc, 0, :], xn[XW][:, kc, :], kc == 0, False, [bw, b_xn[XW]], [p_b])
                for kc in range(8):
                    self.MM(p_ap[:, 0:128], w1a1[:, kc, 1, :], xn[XA][:, kc, :], False, kc == 7, [bw, b_xn[XA]], [p_b])
                self.ACT(hwa[0:64, :], p_ap[0:64, 0:128], AF.Tanh, [p_b], [b_hwa])
                self.CP(K.dve, hwa[64:128, :], p_ap[64:128, 0:128], [p_b], [b_hwa])
                if STAGE < 3:
                    continue
                vps = []
                for hf in range(2):
                    p_ap, p_b = banks.next()
                    for kc in range(8):
                        self.MM(p_ap, xn[XV][:, kc, :], Wv[:, kc, hf * 512:(hf + 1) * 512], kc == 0, kc == 7,
                                [bw, b_xn[XV]], [p_b])
                    vps.append((p_ap, p_b))
                if j == 0:
                    for hf, (p_ap, p_b) in enumerate(vps):
                        self.CP(K.act, Vtm[:, hf * 512:(hf + 1) * 512], p_ap, [p_b], [b_V])
                else:
                    for hf, (p_ap, p_b) in enumerate(vps):
                        self.CP(K.act, vraw[:, hf * 512:(hf + 1) * 512], p_ap, [p_b], [b_vraw])
                    p_ap, p_b = banks.next()
                    for kc in range(8):
                        self.MM(p_ap[:, 0:128], v1p[:, kc, :], xn[XV][:, kc, :], kc == 0, kc == 7, [bw, b_xn[XV]], [p_b])
                    self.CP(K.dve, hv[0:32, :], p_ap[0:32, 0:128], [p_b], [bhv])
                    K.dma(K.sp, vft[:], self.vf[g0:g0 + 128, :], reads=[self.dbuf("vf")], writes=[b_vft])
                    for hf in range(2):
                        p_ap, p_b = banks.next()
                        self.MM(p_ap, hv[:], v2p[:, hf * 512:(hf + 1) * 512], True, True, [bhv, bw], [p_b])
                        self.ACT(gate[:, hf * 512:(hf + 1) * 512], p_ap, AF.Sigmoid, [p_b], [b_gate])
                    self.TT(K.dve, ltmp[:].rearrange("p a b -> p (a b)"), vft[:], vraw[:], ALU.subtract,
                            [b_vft, b_vraw, b_xn[4]], [b_ltmp])
                    self.TT(K.pool, gate[:], gate[:], ltmp[:].rearrange("p a b -> p (a b)"), ALU.mult,
                            [b_gate, b_ltmp], [b_gate])
                    self.TT(K.dve, Vtm[:], gate[:], vraw[:], ALU.add, [b_gate, b_vraw], [b_V])
                if d == 0:
                    vdst = self.vf if j == 0 else self.vsc
                    K.dma(K.sp, vdst[g0:g0 + 128, :], Vtm[:], reads=[b_V],
                          writes=[self.dbuf("vf_w" if j == 0 else "vsc")])
                if STAGE < 4:
                    continue
                for oc in range(8):
                    p_ap, p_b = banks.next()
                    pv = p_ap.rearrange("p (a b) -> p a b", a=4)
                    cs_ = slice(oc * 128, (oc + 1) * 128)
                    for kc in range(8):
                        self.MM(pv[:, 0, :], Wr[:, kc, cs_], xn[XR][:, kc, :], kc == 0, kc == 7, [bw, b_xn[XR]], [p_b])
                    for kc in range(8):
                        self.MM(pv[:, 1, :], Wk[:, kc, cs_], xn[XK][:, kc, :], kc == 0, kc == 7, [bw, b_xn[XK]], [p_b])
                    S4 = DBG.get("S4", 9)
                    if S4 >= 2:
                        self.MM(pv[:, 2, :], w2a2[:, 0, cs_], hwa[:], True, True, [bw, b_hwa], [p_b])
                        self.MM(pv[:, 3, :], w2a2[:, 1, cs_], hwa[:], True, True, [bw, b_hwa], [p_b])
                    if S4 >= 1 or S4 == -1:
                        self.CP(K.act, F["rs"][0][:, oc, :], pv[:, 0, :], [p_b], [F["rs"][1]])
                    if S4 >= 1 or S4 == -2:
                        self.CP(K.act, F["ks"][0][:, oc, :], pv[:, 1, :], [p_b], [F["ks"][1]])
                    if S4 >= 3:
                        self.ACT(F["B1"][0][:, oc, :], pv[:, 2, :], AF.Sigmoid, [p_b, bw], [F["B1"][1]],
                                 bias=prm[:, 6, oc:oc + 1])
                        self.ACT(F["a"][0][:, oc, :], pv[:, 3, :], AF.Sigmoid, [p_b, bw], [F["a"][1]],
                                 bias=prm[:, 7, oc:oc + 1])
                if STAGE < 5:
                    continue
                B1, bB1 = F["B1"]; B2, bB2 = F["B2"]; eP, beP = F["eP"]; eN, beN = F["eN"]
                ks, bks = F["ks"]; rs, brs = F["rs"]; aa, baa = F["a"]; kk, bkk = F["kk"]; rn, brn = F["rn"]
                fl = lambda t_: t_[:].rearrange("p a b -> p (a b)")
                if d == 0:
                    K.op(K.dve, lambda e: e.tensor_tensor_scan(out=fl(B2), data0=fl(segm), data1=fl(B1), initial=0.0,
                                                                op0=ALU.mult, op1=ALU.add), [bB1, bw], [bB2])
                else:
                    K.op(K.dve, lambda e: e.tensor_tensor_scan(out=fl(B2)[:, ::-1], data0=fl(segm)[:, ::-1],
                                                                data1=fl(B1)[:, ::-1], initial=0.0,
                                                                op0=ALU.mult, op1=ALU.add), [bB1, bw], [bB2])
                self.TT(K.pool, B1[:], B2[:], B1[:], ALU.subtract, [bB1, bB2], [bB1])
                self.ACT(eP[:], B2[:], AF.Exp, [bB2], [beP], scale=-CD)
                self.ACT(eN[:], B2[:], AF.Exp, [bB2], [beN], scale=CD)
                self.ACT(B1[:], B1[:], AF.Exp, [bB1], [bB1], scale=-CD)
                for sc_ in range(NSC):
                    tl = sc_ * SC + (SC - 1 if d == 0 else 0)
                    self.CP(K.pool, PC[:, :, sc_], eP[:, :, tl], [beP], [b_PC])
                if STAGE < 6:
                    continue
                self.TT(K.pool, kk[:], ks[:], pbc(8), ALU.mult, [bks, bw], [bkk])
                self.TT(K.pool, sq[:], kk[:], kk[:], ALU.mult, [bkk], [b_sq])
                for hf in range(2):
                    p_ap, p_b = banks.next()
                    pv = p_ap.rearrange("p (a b) -> p a b", a=4)
                    for a_ in range(4):
                        self.MM(pv[:, a_, :], blk2[:], sq[:, hf * 4 + a_, :], True, True, [b_sq, bw], [p_b])
                    self.ACT(rn[:, hf * 4:(hf + 1) * 4, :], pv, AF.Ln, [p_b], [brn])
                self.ACT(rn[:], rn[:], AF.Exp, [brn], [brn], scale=-0.5)
                self.TT(K.dve, kk[:], kk[:], rn[:], ALU.mult, [bkk, brn], [bkk])
                if STAGE < 7:
                    continue
                self.TT(K.pool, B2[:], ks[:], pbc(9), ALU.mult, [bks, bw, beP, beN], [bB2])
                self.STT(B2[:], aa[:], -1.0, B2[:], ALU.add, ALU.mult, [baa, bB2], [bB2])
                self.TT(K.pool, B2[:], B2[:], ks[:], ALU.add, [bB2, bks], [bB2])
                self.TT(K.dve, aa[:], kk[:], aa[:], ALU.mult, [bkk, baa], [baa])
                self.TT(K.pool, KTb[:], B2[:], eN[:], ALU.mult, [bB2, beN], [b_KT])
                self.TT(K.dve, BTb[:], aa[:], eN[:], ALU.mult, [baa, beN], [b_BT])
                self.STT(ATb[:], kk[:], -1.0, B1[:], ALU.mult, ALU.mult, [bkk, bB1], [b_AT])
                self.CP(K.act, ARz[0:64, :, 0, 0:128], ATb[0:64], [b_AT], [b_AR])
                self.CP(K.act, ARz[64:128, :, 1, 0:128], ATb[64:128], [b_AT], [b_AR])
                self.TT(K.pool, ARz[0:64, :, 0, 128:256], rs[0:64], eP[0:64], ALU.mult, [brs, beP], [b_AR])
                self.TT(K.pool, ARz[64:128, :, 1, 128:256], rs[64:128], eP[64:128], ALU.mult, [brs, beP], [b_AR])
                if STAGE < 8:
                    continue
                self.TT(K.pool, rs[:], rs[:], pbc(10), ALU.mult, [brs, bw], [brs])
                self.TT(K.dve, rkk[:], rs[:], B2[:], ALU.mult, [brs, bB2], [b_rkk])
                p_ap, p_b = banks.next()
                for oc in range(8):
                    self.MM(p_ap[:, 2 * oc:2 * oc + 2], rkk[:, oc, :], blkc[:], True, True, [b_rkk, bw], [p_b])
                self.CP(K.dve, bon[:], p_ap[:, 0:16], [p_b], [b_bon])
                K.dma(K.sp, self.bonsc[d, g0:g0 + 128, :], bon[:], reads=[b_bon], writes=[self.dbuf("bonsc")])
                if STAGE < 9:
                    continue
                for srcT, b_src, kind in ((ATb, b_AT, 0), (BTb, b_BT, 1), (KTb, b_KT, 2)):
                    p_ap, p_b = banks.next()
                    pb16 = p_ap.bitcast(BF16)
                    for oc in range(8):
                        self.TR(pb16[:, oc * 128:(oc + 1) * 128], srcT[:, oc, :], [b_src], [p_b])
                    if kind == 0:
                        self.CP(K.act, X[0][:, :, 0:64], pb16.rearrange("p (h k) -> p h k", k=64), [p_b],
                                [bb for bb in b_X[0]])
                    elif kind == 1:
                        self.CP(K.dve, Btm[:], pb16, [p_b], [b_Btm])
                    else:
                        self.CP(K.act, Ktm[:], pb16, [p_b], [b_Ktm])
                if STAGE < 10:
                    continue
                if d == 1:
                    K.dma(K.sp, ysb[:], self.ysc[g0:g0 + 128, :], reads=[self.dbuf(("ysc", g0))], writes=[b_y])
                else:
                    self.MS(K.pool, ysb[:], 0.0, [b_y])
                for sc_ in range(NSC):
                    K.op(K.pool if sc_ % 2 == 0 else K.act,
                         (lambda e, sc_=sc_: e.tensor_scalar(out=Vz[sc_][:], in0=Vtm[:], scalar1=rm[:, sc_:sc_ + 1],
                                                             scalar2=None, op0=ALU.mult)) if sc_ % 2 == 0 else
                         (lambda e, sc_=sc_: e.activation(out=Vz[sc_][:], in_=Vtm[:], func=AF.Copy,
                                                          scale=rm[:, sc_:sc_ + 1])),
                         [b_V, bw], [b_Vz])
                for gh in range(2):
                    H0 = gh * 8
                    Q0 = gh * 4
                    for hh_ in range(8):
                        h = H0 + hh_
                        oc, par = h // 2, h % 2
                        p_ap, p_b = banks.next()
                        self.MM(p_ap[:, 0:256], BTb[:, oc, :], ARz[:, oc, par, :], True, True, [b_BT, b_AR], [p_b])
                        self.MM(p_ap[:, 256:512], KTb[:, oc, :], ARz[:, oc, par, :], True, True, [b_KT, b_AR], [p_b])
                        self.TT(K.dve, MMs[:, hh_, :, :], p_ap.rearrange("p (a b) -> p a b", a=4), msk[:, 0:4, :],
                                ALU.mult, [p_b, bw], [b_MM[hh_]])
                    for s4 in range(2):
                        p_ap, p_b = banks.next()
                        pv = p_ap.rearrange("p (a b) -> p a b", a=4)
                        for a_ in range(4):
                            hh_ = s4 * 4 + a_
                            h = H0 + hh_
                            oc, par = h // 2, h % 2
                            self.MM(pv[:, a_, :], ARz[:, oc, par, 0:128], BTb[:, oc, :], True, True, [b_AR, b_BT], [p_b])
                        self.TT(K.dve, L0s[:, s4 * 4:(s4 + 1) * 4, :], pv,
                                msk[:, 4, :].unsqueeze(1).to_broadcast([128, 4, 128]), ALU.mult, [p_b, bw], [b_L0[s4]])
                    p_ap, p_b = banks.next()
                    pv = p_ap.rearrange("p (h v) -> p h v", v=64)
                    for hh_ in range(8):
                        h = H0 + hh_
                        self.MM(pv[:, hh_, :], MMs[:, hh_, 2, :], Vtm[:, h * 64:(h + 1) * 64], True, True,
                                [b_MM[hh_], b_V], [p_b])
                    self.CP(K.act, X[0][:, H0:H0 + 8, 64:128], pv, [p_b], [b_X[0][2 * gh], b_X[0][2 * gh + 1]])
                    NLEV = 5
                    def Nf(lv, hh_):
                        return MMs[:, hh_, 0, :] if lv == 0 else LN[lv % 2][:, hh_, 1, :]

                    def Lf(lv, hh_):
                        return L0s[:, hh_, :] if lv == 0 else LN[lv % 2][:, hh_, 0, :]

                    def Nb(lv, hh_):
                        return b_MM[hh_] if lv == 0 else b_LN[lv % 2][hh_ // 2]

                    def Lb(lv, hh_):
                        return b_L0[hh_ // 4] if lv == 0 else b_LN[lv % 2][hh_ // 2]

                    for lv in range(NLEV):
                        xs_i, xd_i = lv % 2, (lv + 1) % 2
                        for s4 in range(2):
                            p_ap, p_b = banks.next()
                            pv = p_ap.rearrange("p (a b) -> p a b", a=4)
                            bxs = b_X[xs_i][2 * gh + s4]
                            bxd = b_X[xd_i][2 * gh + s4]
                            for a_ in range(4):
                                hh_ = s4 * 4 + a_
                                xin = X[xs_i][:, H0 + hh_, :]
                                self.MM(pv[:, a_, :], self.ident[:], xin, True, False, [bxs, self.b_const], [p_b])
                                self.MM(pv[:, a_, :], Nf(lv, hh_), xin, False, True, [bxs, Nb(lv, hh_)], [p_b])
                            hs = slice(H0 + s4 * 4, H0 + s4 * 4 + 4)
                            if lv < NLEV - 1:
                                self.CP(K.act if s4 == 0 else K.dve, X[xd_i][:, hs, :], pv, [p_b], [bxd])
                            else:
                                self.CP(K.dve, X[xd_i][:, hs, 64:128], pv[:, :, 64:128], [p_b], [bxd])
                                wz4 = Wz[:, s4 * 4:(s4 + 1) * 4, :].rearrange("p (q r) c -> p q r c", r=2)
                                pv4 = pv.rearrange("p (q r) c -> p q r c", r=2)
                                self.CP(K.dve, wz4[:, :, 0, 0:64], pv4[:, :, 0, 0:64], [p_b], [b_Wz])
                                self.CP(K.dve, wz4[:, :, 1, 64:128], pv4[:, :, 1, 0:64], [p_b], [b_Wz])
                        if lv < NLEV - 1:
                            nl = (lv + 1) % 2
                            for s2 in range(4):
                                p_ap, p_b = banks.next()
                                pv = p_ap.rearrange("p (h t b) -> p h t b", h=2, t=2)
                                for a_ in range(2):
                                    hh_ = s2 * 2 + a_
                                    self.MM(pv[:, a_, 0, :], Nf(lv, hh_), Lf(lv, hh_), True, True,
                                            [Nb(lv, hh_), Lb(lv, hh_)], [p_b])
                                    self.MM(pv[:, a_, 1, :], Lf(lv, hh_), Nf(lv, hh_), True, True,
                                            [Nb(lv, hh_), Lb(lv, hh_)], [p_b])
                                self.CP(K.act if s2 % 2 == 0 else K.dve, LN[nl][:, s2 * 2:s2 * 2 + 2, :, :], pv, [p_b],
                                        [b_LN[nl][s2]])
                    XF = X[NLEV % 2]
                    bXF = [b_X[NLEV % 2][2 * gh], b_X[NLEV % 2][2 * gh + 1]]
                    p_ap, p_b = banks.next()
                    pb16 = p_ap.bitcast(BF16)
                    for hh_ in range(8):
                        self.TR(pb16[:, hh_ * 128:(hh_ + 1) * 128], Wz[:, hh_, :], [b_Wz], [p_b])
                    self.CP(K.act, WTz[:], pb16.rearrange("p (h t) -> p h t", t=128), [p_b], [b_WTz])
                    for sc_ in (range(NSC) if d == 0 else reversed(range(NSC))):
                        pcb = PC[:, Q0:Q0 + 4, sc_].unsqueeze(2).to_broadcast([128, 4, 128])
                        p_ap, p_b = banks.next()
                        pv = p_ap.rearrange("p (h v) -> p h v", v=64)
                        for hh_ in range(8):
                            h = H0 + hh_
                            q, par = h // 2, h % 2
                            self.MM(pv[:, hh_, :], WTz[:, hh_, :], Ab[:, q, par * 64:(par + 1) * 64], True, True,
                                    [b_WTz, b_Ab], [p_b])
                        self.TT(K.dve, Us[:], pv, XF[:, H0:H0 + 8, 64:128], ALU.add, [p_b] + bXF, [b_U])
                        self.TS(K.pool, Us[:], Us[:], rm[:, sc_:sc_ + 1], None, ALU.mult, None, [b_U, bw], [b_U])
                        py_ap, py_b = banks.next()
                        pyv = py_ap.rearrange("p (h v) -> p h v", v=64)
                        for hh_ in range(8):
                            h = H0 + hh_
                            q, par = h // 2, h % 2
                            self.MM(pyv[:, hh_, :], ARz[:, q, par, 128:256], Ab[:, q, par * 64:(par + 1) * 64],
                                    True, False, [b_AR, b_Ab], [py_b])
                            self.MM(pyv[:, hh_, :], MMs[:, hh_, 1, :], Us[:, hh_, :], False, False,
                                    [b_MM[hh_], b_U], [py_b])
                            self.MM(pyv[:, hh_, :], MMs[:, hh_, 3, :], Vtm[:, h * 64:(h + 1) * 64], False, True,
                                    [b_MM[hh_], b_V], [py_b])
                        ycols = slice(H0 * 64, H0 * 64 + 512)
                        self.STT(ysb[:, ycols], py_ap, rm[:, sc_:sc_ + 1], ysb[:, ycols], ALU.mult, ALU.add,
                                 [py_b, b_y, bw], [b_y])
                        p_ap, p_b = banks.next()
                        pv = p_ap.rearrange("p (a b) -> p a b", a=4)
                        for a_ in range(4):
                            q = Q0 + a_
                            for par in range(2):
                                h = 2 * q + par
                                o_ = pv[:, a_, par * 64:(par + 1) * 64]
                                self.MM(o_, Btm[:, q * 128:(q + 1) * 128], Us[:, 2 * a_ + par, :], True, False,
                                        [b_Btm, b_U], [p_b])
                                self.MM(o_, Ktm[:, q * 128:(q + 1) * 128], Vz[sc_][:, h * 64:(h + 1) * 64], False, True,
                                        [b_Ktm, b_Vz], [p_b])
                        self.TT(K.dve, Af[:, Q0:Q0 + 4, :], pv, Af[:, Q0:Q0 + 4, :], ALU.add, [p_b, b_Af], [b_Af])
                        self.TT(K.pool, Af[:, Q0:Q0 + 4, :], Af[:, Q0:Q0 + 4, :], pcb, ALU.mult, [b_Af, b_PC], [b_Af])
                        self.CP(K.act, Ab[:, Q0:Q0 + 4, :], Af[:, Q0:Q0 + 4, :], [b_Af], [b_Ab])
                K.dma(K.sp, self.ysc[g0:g0 + 128, :], ysb[:], reads=[b_y], writes=[self.dbuf(("ysc", g0))])
            K.barrier()

    def rwkv_out(self, i, j, x_src, c_src, x_dst, c_dst):
        K = self.K
        with ExitStack() as st:
            sb = lambda name, shape, dt: self.sb(name, shape, dt, st)
            Wg = sb("ro_Wg", [128, 8, D], BF16)
            Wo = sb("ro_Wo", [128, 8, D], BF16)
            bw = Buf()
            for kc in range(8):
                K.dma(K.pool, Wg[:, kc, :], self.rw_w_rkvg[j, 3, kc * 128:(kc + 1) * 128, :], writes=[bw])
                K.dma(K.pool, Wo[:, kc, :], self.rw_w_out[j, kc * 128:(kc + 1) * 128, :], writes=[bw])
            mu = sb("ro_mu", [128, 8], F32)
            K.dma(K.sp, mu[:], self.rw_mu_fm[j, :, 5, :], writes=[bw])
            lw = sb("ro_lw", [128, D], F32)
            lb = sb("ro_lb", [128, D], F32)
            K.dma(K.sp, lw[:], self.rw_lnx_w[j].partition_broadcast(128), writes=[bw])
            K.dma(K.sp, lb[:], self.rw_lnx_b[j].partition_broadcast(128), writes=[bw])
            hhR = Ring([sb("ro_hh%d" % k, [128, 8, 130], BF16)[:] for k in range(2)])
            xs_t = sb("ro_xs", [128, 8, 128], BF16); b_xs = Buf()
            xx_t = sb("ro_xx", [128, 8, 128], BF16); b_xx = Buf()
            ltmp = sb("ro_ltmp", [128, 8, 128], F32); b_lt = Buf()
            xg = sb("ro_xg", [128, 8, 128], BF16); b_xg = Buf()
            gs = sb("ro_g", [128, D], BF16); b_g = Buf()
            yR = Ring([sb("ro_y%d" % k, [128, D], F32)[:] for k in range(2)])
            vR = Ring([sb("ro_v%d" % k, [128, D], BF16)[:] for k in range(2)])
            bnR = Ring([sb("ro_bn%d" % k, [128, 2, 16], F32)[:] for k in range(2)])
            sqt = sb("ro_sq", [128, D], F32); b_sq = Buf()
            st1 = sb("ro_st", [128, 4, 16], F32); b_st = Buf()
            og = sb("ro_og", [128, D], BF16); b_og = Buf()
            ogT = sb("ro_ogT", [128, 8, 128], BF16); b_ogT = Buf()
            P = self.post_alloc(i, st)
            banks = Ring([self.ps("ro_ps%d" % k, [128, 512], F32, st)[:] for k in range(6)])
            mub = mu[:].unsqueeze(2).to_broadcast([128, 8, 128])
            cur_w = None
            for seg_, n in self.rw_chunks(0):
                if seg_ == "c" and c_dst is None:
                    continue
                w = 1 if seg_ == "c" else 0
                if w != cur_w:
                    K.dma(K.sp, P["gb"][:], self.gb[i, w], reads=[self.dbuf("gb")], writes=[P["bgb"]])
                    cur_w = w
                g0 = (0 if seg_ == "c" else TC) + n * 128
                hh, hh_b = self.load_hh(hhR, seg_, n)
                hc = hh[:, :, 1:129]
                self.TT(K.pool, xs_t[:], hh[:, :, 0:128], hh[:, :, 2:130], ALU.add, [hh_b], [b_xs])
                self.STT(xx_t[:], xs_t[:], 0.5, hc, ALU.mult, ALU.subtract, [b_xs, hh_b], [b_xx])
                self.TT(K.pool, ltmp[:], xx_t[:], mub, ALU.mult, [b_xx, bw], [b_lt])
                self.TT(K.dve, xg[:], ltmp[:], hc, ALU.add, [b_lt, hh_b], [b_xg])
                for hf in range(2):
                    p_ap, p_b = banks.next()
                    for kc in range(8):
                        self.MM(p_ap, xg[:, kc, :], Wg[:, kc, hf * 512:(hf + 1) * 512], kc == 0, kc == 7, [bw, b_xg], [p_b])
                    self.ACT(gs[:, hf * 512:(hf + 1) * 512], p_ap, AF.Silu, [p_b], [b_g])
                y_ap, y_b = yR.next()
                v_ap, v_b = vR.next()
                bn_ap, bn_b = bnR.next()
                K.dma(K.sp, y_ap, self.ysc[g0:g0 + 128, :], reads=[self.dbuf(("ysc", g0))], writes=[y_b])
                K.dma(K.sp, v_ap, (self.vf if j == 0 else self.vsc)[g0:g0 + 128, :],
                      reads=[self.dbuf("vf_w"), self.dbuf("vsc")], writes=[v_b])
                for dd in range(2):
                    K.dma(K.sp, bn_ap[:, dd, :], self.bonsc[dd, g0:g0 + 128, :], reads=[self.dbuf("bonsc")], writes=[bn_b])
                y3 = y_ap.rearrange("p (h v) -> p h v", v=64)
                K.op(K.dve, lambda e, y3=y3: e.tensor_reduce(out=st1[:, 0, :], in_=y3, axis=AX.X, op=ALU.add),
                     [y_b], [b_st])
                self.TT(K.pool, sqt[:], y_ap, y_ap, ALU.mult, [y_b], [b_sq])
                K.op(K.dve, lambda e: e.tensor_reduce(out=st1[:, 1, :], in_=sqt[:].rearrange("p (h v) -> p h v", v=64),
                                                      axis=AX.X, op=ALU.add), [b_sq], [b_st])
                self.TS(K.dve, st1[:, 0, :], st1[:, 0, :], 1.0 / 64, None, ALU.mult, None, [b_st], [b_st])
                self.TT(K.dve, st1[:, 2, :], st1[:, 0, :], st1[:, 0, :], ALU.mult, [b_st], [b_st])
                self.STT(st1[:, 1, :], st1[:, 1, :], 1.0 / 64, st1[:, 2, :], ALU.mult, ALU.subtract, [b_st], [b_st])
                self.TS(K.dve, st1[:, 1, :], st1[:, 1, :], 64e-5, None, ALU.add, None, [b_st], [b_st])
                self.TT(K.pool, st1[:, 2, :], st1[:, 1, :], self.mhalf[:].to_broadcast([128, 16]), ALU.pow,
                        [b_st, self.b_const], [b_st])
                self.TT(K.pool, st1[:, 3, :], bn_ap[:, 0, :], bn_ap[:, 1, :], ALU.add, [bn_b], [b_st])
                bc = lambda k: st1[:, k, :].unsqueeze(2).to_broadcast([128, 16, 64])
                self.TT(K.dve, y3, y3, bc(0), ALU.subtract, [y_b, b_st], [y_b])
                self.TT(K.pool, y3, y3, bc(2), ALU.mult, [y_b, b_st], [y_b])
                self.TT(K.dve, y_ap, y_ap, lw[:], ALU.mult, [y_b, bw], [y_b])
                self.TT(K.pool, y_ap, y_ap, lb[:], ALU.add, [y_b, bw], [y_b])
                sq3 = sqt[:].rearrange("p (h v) -> p h v", v=64)
                self.TT(K.dve, sq3, v_ap.rearrange("p (h v) -> p h v", v=64), bc(3), ALU.mult, [v_b, b_st], [b_sq])
                self.TT(K.pool, y_ap, y_ap, sqt[:], ALU.add, [y_b, b_sq], [y_b])
                self.TT(K.dve, og[:], y_ap, gs[:], ALU.mult, [y_b, b_g], [b_og])
                t_ap, t_b = banks.next()
                tv = t_ap.bitcast(BF16).rearrange("p (a b) -> p a b", b=128)
                for oc in range(8):
                    self.TR(tv[:, oc, :], og[:, oc * 128:(oc + 1) * 128], [b_og], [t_b])
                self.CP(K.act, ogT[:], tv, [t_b], [b_ogT])
                halves = []
                for hf in range(2):
                    p_ap, p_b = banks.next()
                    for kc in range(8):
                        self.MM(p_ap, ogT[:, kc, :], Wo[:, kc, hf * 512:(hf + 1) * 512], kc == 0, kc == 7, [b_ogT, bw], [p_b])
                    halves.append((p_ap, p_b))
                if seg_ == "c":
                    self.post_tile(i, 1, halves, c_src, c_dst, n * 128, P, False)
                else:
                    self.post_tile(i, 0, halves, x_src, x_dst, n * 128, P, x_dst is self.y_out)
            K.barrier()


def _fm(v):
    v = np.asarray(v, np.float32)
    lead = v.shape[:-1]
    r = v.reshape(*lead, 8, 128)
    return np.ascontiguousarray(np.moveaxis(r, -1, 0))


def rw_host_layout(inp):
    f = lambda a: np.asarray(a, np.float32)
    z = np.zeros
    w1, a1, w2, a2 = f(inp["rw_w1"]), f(inp["rw_a1"]), f(inp["rw_w2"]), f(inp["rw_a2"])
    w1pad = z((2, 2, 128, 8, 128), np.float32)
    a1pad = z((2, 2, 128, 8, 128), np.float32)
    w1pad[..., 0:64] = np.transpose(w1.reshape(2, 2, 8, 128, 64), (0, 1, 3, 2, 4))
    a1pad[..., 64:128] = np.transpose(a1.reshape(2, 2, 8, 128, 64), (0, 1, 3, 2, 4))
    w2pad = z((2, 2, 128, D), np.float32)
    a2pad = z((2, 2, 128, D), np.float32)
    w2pad[:, :, 0:64, :] = w2
    a2pad[:, :, 64:128, :] = a2
    v1pad = z((128, 8, 128), np.float32)
    v1pad[:, :, 0:32] = np.transpose(f(inp["rw_v1"])[0].reshape(8, 128, 32), (1, 0, 2))
    v2pad = z((128, D), np.float32)
    v2pad[0:32] = f(inp["rw_v2"])[0]
    v2pad[32] = f(inp["rw_v0"])[0]

    def fm(v):
        v = f(v)
        return np.ascontiguousarray(np.swapaxes(v.reshape(*v.shape[:-1], 8, 128), -1, -2))

    mu_fm = np.ascontiguousarray(np.transpose(fm(inp["rw_mu"]), (0, 2, 1, 3)))
    s_ = np.arange(128)[:, None]
    t_ = np.arange(128)[None, :]
    same = (s_ // SC == t_ // SC)
    masks = z((2, 128, 5, 128), np.float32)
    masks[0, :, 0] = masks[0, :, 2] = (s_ < t_) & same
    masks[0, :, 1] = masks[0, :, 3] = (s_ <= t_) & same
    masks[0, :, 4] = (t_ < s_) & same
    masks[1, :, 0] = masks[1, :, 2] = (s_ > t_) & same
    masks[1, :, 1] = masks[1, :, 3] = (s_ >= t_) & same
    masks[1, :, 4] = (t_ > s_) & same
    seg = np.ones((2, 128, 8, 128), np.float32)
    seg[0, :, :, 0::SC] = 0.0
    seg[1, :, :, SC - 1::SC] = 0.0
    rowmask = (np.arange(128)[:, None] // SC == np.arange(NSC)[None, :]).astype(np.float32)
    blk2 = (np.arange(128)[:, None] // 64 == np.arange(128)[None, :] // 64).astype(np.float32)
    blkc = (np.arange(128)[:, None] // 64 == np.arange(2)[None, :]).astype(np.float32)
    return {
        "rw_w_rkvg": np.ascontiguousarray(f(inp["rw_w_rkvg"])),
        "rw_w_out": np.ascontiguousarray(f(inp["rw_w_out"])),
        "rw_w1pad": w1pad, "rw_a1pad": a1pad, "rw_w2pad": w2pad, "rw_a2pad": a2pad,
        "rw_v1pad": v1pad, "rw_v2pad": v2pad,
        "rw_mu_fm": mu_fm,
        "rw_w0_fm": fm(inp["rw_w0"]), "rw_a0_fm": fm(inp["rw_a0"]),
        "rw_kk_fm": fm(inp["rw_k_k"]), "rw_ka_fm": fm(inp["rw_k_a"]),
        "rw_rk_fm": fm(f(inp["rw_r_k"]).reshape(2, D)),
        "rw_lnx_w": np.ascontiguousarray(f(inp["rw_lnx_w"])[:, None, :]),
        "rw_lnx_b": np.ascontiguousarray(f(inp["rw_lnx_b"])[:, None, :]),
        "rw_masks": masks, "rw_seg": seg, "rw_blk2": blk2, "rw_blkc": blkc, "rw_rowmask": rowmask,
    }


def make_in_maps(inp, layers=(0, 1, 2, 3)):
    f = lambda a: np.ascontiguousarray(np.asarray(a, np.float32))
    dr_idx, dc_idx, mask = na_index_tables()
    rpb = np.asarray(inp["na_rpb"], np.float32)
    bias_g = rpb[:, :, dr_idx, dc_idx]
    bias_g = np.ascontiguousarray(np.transpose(bias_g, (0, 3, 1, 2, 4)))
    mask_l = np.ascontiguousarray(np.transpose(mask, (1, 0, 2)))
    ada_b = f(inp["ada_b"])
    shared = {
        "ada_w": f(inp["ada_w"]),
        "ada_b_fm": np.ascontiguousarray(np.transpose(
            ada_b[:, :2048].reshape(DEPTH, 16, 128), (2, 0, 1))),
        "ada_b_gate": np.ascontiguousarray(ada_b[:, None, 2048:]),
        "pre_g_fm": np.ascontiguousarray(np.transpose(f(inp["pre_g"]).reshape(DEPTH, 8, 128), (2, 0, 1))),
        "post_g_row": np.ascontiguousarray(f(inp["post_g"])[:, None, :]),
        "na_w_in": f(inp["na_w_in"]),
        "na_bqk": np.ascontiguousarray(np.transpose(
            f(inp["na_b_in"])[:, :2048].reshape(2, 2, 8, 128), (0, 2, 1, 3)).reshape(2, 8, 256)),
        "na_bvg": np.ascontiguousarray(f(inp["na_b_in"])[:, 2048:].reshape(2, 4, 512)),
        "na_w_out": f(inp["na_w_out"]),
        "na_bias_g": bias_g,
        "na_mask": mask_l,
    }
    shared.update(rw_host_layout(inp))
    maps = []
    x = np.asarray(inp["x"], np.float32)
    ctx = np.asarray(inp["ctx"], np.float32)
    c = np.asarray(inp["c"], np.float32)
    cc = np.asarray(inp["c_ctx"], np.float32)
    for b in range(x.shape[0]):
        c_fm = np.stack([c[b].reshape(8, 128).T, cc.reshape(8, 128).T], axis=-1)
        m = dict(shared)
        m["x"] = np.ascontiguousarray(x[b])
        m["ctx"] = np.ascontiguousarray(ctx[b])
        m["c_fm"] = np.ascontiguousarray(c_fm.astype(np.float32))
        maps.append(m)
    return maps


_CACHE = {}


def run(inp, layers=(0, 1, 2, 3), cores=None):
    key = tuple(layers)
    if key not in _CACHE:
        _CACHE[key] = Prog(layers).build()
    nc = _CACHE[key]
    maps = make_in_maps(inp, layers)
    if cores is not None:
        maps = maps[:cores]
    res = run_bass_kernel_spmd(nc, maps, core_ids=list(range(len(maps))))
    if DBG.get("DUMP"):
        DBG["res"] = res.results
    return np.stack([np.asarray(r["y"], np.float32) for r in res.results], axis=0)


def kernel(**inputs):
    return run(inputs)
```
